# Optimizing a Trainium2 kernel written in Bass

```python
import jax, jax.numpy as jnp
from jax import lax
import numpy as np

D_MODEL = 1024
BATCH = 8
SEQ = 2048
DEPTH = 1
DEC_BATCH = 128
DEC_SEQ = 1
PAST_LEN = 16384
PAGE_SIZE = 128

D_MIX = 2 * D_MODEL
D_A = D_MIX // 2
D_B = D_MIX - D_A
CHUNK = 128
HD_A = 128
H_A = D_A // HD_A
H_B = 8
CONV_W = 3
EPS = 1e-5
SPLIT_SIZES = (D_A, D_A, D_A, D_B, D_B, D_B, D_B)
SPLIT_IDX = tuple(int(i) for i in np.cumsum(SPLIT_SIZES)[:-1])
D_IN = sum(SPLIT_SIZES)

kernel_name = "hymba_chunk_gmlp_shortconv_step"


def rmsnorm(x, g):
    xf = x.astype(jnp.float32)
    inv = lax.rsqrt(jnp.mean(xf * xf, axis=-1, keepdims=True) + EPS)
    return (xf * inv * g.astype(jnp.float32)).astype(x.dtype)


def chunk_mix(v, w_s, b_s):
    n, t = v.shape[0], v.shape[1]
    pad = (-t) % CHUNK
    vp = jnp.pad(v, ((0, 0), (0, pad), (0, 0), (0, 0)))
    nc = (t + pad) // CHUNK
    vp = vp.reshape(n, nc, CHUNK, H_A, HD_A)
    mask = jnp.tril(jnp.ones((CHUNK, CHUNK), dtype=w_s.dtype))
    w = w_s * mask[None]
    out = jnp.einsum('hts,ncshd->ncthd', w, vp)
    out = out + jnp.transpose(b_s)[None, None, :, :, None]
    return out.reshape(n, nc * CHUNK, H_A, HD_A)[:, :t]


def mixer_layer(x, conv_state, g_norm, w_in, w_s, b_s, g_v, conv_w, w_out):
    n, t = x.shape[0], x.shape[1]
    h = rmsnorm(x, g_norm)
    proj = jnp.einsum('btd,de->bte', h, w_in)
    u, v, z_a, x_b, gate_b, gate_c, z_b = jnp.split(proj, SPLIT_IDX, axis=-1)
    u = jax.nn.gelu(u, approximate=False)
    v = rmsnorm(jax.nn.gelu(v, approximate=False), g_v)
    mixed = chunk_mix(v.reshape(n, t, H_A, HD_A), w_s, b_s).reshape(n, t, D_A)
    out_a = u * mixed * jax.nn.silu(z_a)
    hb = gate_c * x_b
    hb_ext = jnp.concatenate([conv_state.astype(hb.dtype), hb], axis=1)
    conv = hb_ext[:, 0:t] * conv_w[0]
    for k in range(1, CONV_W):
        conv = conv + hb_ext[:, k:k + t] * conv_w[k]
    out_b = gate_b * conv * jax.nn.silu(z_b)
    mix = jnp.concatenate([out_a, out_b], axis=-1)
    y = x + jnp.einsum('bte,ed->btd', mix, w_out)
    new_conv = hb_ext[:, t:]
    return y, new_conv, v


def setup_inputs(seed: int = 0) -> dict:
    key = jax.random.key(seed)
    ks = jax.random.split(key, 12)
    f32 = jnp.float32
    x_prompt = jax.random.normal(ks[0], (BATCH, SEQ, D_MODEL), f32)
    x_sample = jax.random.normal(ks[1], (DEC_BATCH, DEC_SEQ, D_MODEL), f32)
    state_conv = jax.random.normal(ks[2], (DEPTH, DEC_BATCH, CONV_W - 1, D_B), f32)
    g_norm = 1.0 + 0.02 * jax.random.normal(ks[3], (DEPTH, D_MODEL), f32)
    w_in = jax.random.normal(ks[4], (DEPTH, D_MODEL, D_IN), f32) * D_MODEL ** -0.5
    w_s = jax.random.normal(ks[5], (DEPTH, H_A, CHUNK, CHUNK), f32) * (0.5 * CHUNK ** -0.5)
    b_s = 1.0 + 0.1 * jax.random.normal(ks[6], (DEPTH, H_A, CHUNK), f32)
    g_v = 1.0 + 0.02 * jax.random.normal(ks[7], (DEPTH, D_A), f32)
    conv_w = jax.random.normal(ks[8], (DEPTH, CONV_W, D_B), f32) * CONV_W ** -0.5
    w_out = jax.random.normal(ks[9], (DEPTH, D_MIX, D_MODEL), f32) * D_MIX ** -0.5
    g_final = 1.0 + 0.02 * jax.random.normal(ks[10], (D_MODEL,), f32)
    return {"x_prompt": x_prompt, "x_sample": x_sample, "state_conv": state_conv,
            "g_norm": g_norm, "w_in": w_in, "w_s": w_s, "b_s": b_s, "g_v": g_v,
            "conv_w": conv_w, "w_out": w_out, "g_final": g_final}


def reference(x_prompt, x_sample, state_conv, g_norm, w_in, w_s, b_s, g_v, conv_w, w_out, g_final):
    xp, xs = x_prompt, x_sample
    conv_p_list, conv_s_list, v_s_list = [], [], []
    for l in range(DEPTH):
        zero_state = jnp.zeros((xp.shape[0], CONV_W - 1, D_B), xp.dtype)
        xp, conv_p, _ = mixer_layer(xp, zero_state, g_norm[l], w_in[l], w_s[l], b_s[l],
                                    g_v[l], conv_w[l], w_out[l])
        xs, conv_s, v_s = mixer_layer(xs, state_conv[l], g_norm[l], w_in[l], w_s[l], b_s[l],
                                      g_v[l], conv_w[l], w_out[l])
        conv_p_list.append(conv_p)
        conv_s_list.append(conv_s)
        v_s_list.append(v_s)
    y_prompt = rmsnorm(xp, g_final)
    y_sample = rmsnorm(xs, g_final)
    state_conv_prompt = jnp.stack(conv_p_list, axis=0)
    state_conv_sample = jnp.stack(conv_s_list, axis=0)
    state_v_sample = jnp.stack(v_s_list, axis=0)
    return (y_prompt, y_sample, state_conv_prompt, state_conv_sample, state_v_sample)
```

```python
from contextlib import ExitStack

import numpy as np
import concourse.bass as bass
import concourse.mybir as mybir
from concourse.bass_utils import run_bass_kernel_spmd

F32 = mybir.dt.float32
F32R = mybir.dt.float32r
BF16 = mybir.dt.bfloat16
AF = mybir.ActivationFunctionType
ALU = mybir.AluOpType

EXACT = False
MMDT = F32 if EXACT else F32R
SAME_ENGINE_SYNC = True

N_CORES = 8
D = 1024
SEQ = 2048
NS = 16
NBLK = 4
NT = 512
EPS = 1e-5
RING = 4

ENGS = ("sync", "scalar", "vector", "gpsimd", "tensor")


class Buf:
    __slots__ = ("name", "writer", "readers")

    def __init__(self, name):
        self.name = name
        self.writer = None
        self.readers = []


class Op:
    __slots__ = ("eng", "fn", "deps", "dma_sem", "tok", "milestone", "idx")

    def __init__(self, eng, fn, dma_sem):
        self.eng = eng
        self.fn = fn
        self.deps = []
        self.dma_sem = dma_sem
        self.tok = None
        self.milestone = False


class Plan:
    def __init__(self):
        self.q = {e: [] for e in ENGS}
        self.dma_count = {}

    def add(self, eng, fn, reads=(), writes=(), dma_sem=None):
        op = Op(eng, fn, dma_sem)
        deps = []
        for b in reads:
            if b.writer is not None:
                deps.append(b.writer)
        for b in writes:
            if b.writer is not None:
                deps.append(b.writer)
            deps.extend(b.readers)
        for b in reads:
            b.readers.append(op)
        for b in writes:
            b.writer = op
            b.readers = []
        seen = set()
        last = {}
        for d in deps:
            if d is op or id(d) in seen:
                continue
            seen.add(id(d))
            if d.dma_sem is not None:
                op.deps.append(d)
                continue
            if d.eng == eng and (eng == "tensor" or not SAME_ENGINE_SYNC):
                continue
            if d.eng not in last or last[d.eng].idx < d.idx:
                last[d.eng] = d
        for d in last.values():
            op.deps.append(d)
            d.milestone = True
        if dma_sem is not None:
            key = id(dma_sem)
            self.dma_count[key] = self.dma_count.get(key, 0) + 16
            op.tok = (dma_sem, self.dma_count[key])
        op.idx = len(self.q[eng])
        self.q[eng].append(op)
        return op

    def unify(self, ops):
        v = max(o.tok[1] for o in ops)
        for o in ops:
            o.tok = (o.tok[0], v)

    def finalize(self, tick_sems):
        for e in ENGS:
            t = 0
            for op in self.q[e]:
                if op.dma_sem is None and op.milestone:
                    t += 1
                    op.tok = (tick_sems[e], t)

    def emit(self, e, eng):
        waited = {}
        for op in self.q[e]:
            need = {}
            for d in op.deps:
                s, v = d.tok
                if waited.get(id(s), 0) >= v:
                    continue
                if need.get(id(s), (None, 0))[1] < v:
                    need[id(s)] = (s, v)
            for s, v in need.values():
                eng.wait_ge(s, v)
                waited[id(s)] = v
            ins = op.fn(eng)
            if op.dma_sem is not None:
                ins.then_inc(op.dma_sem, 16)
            elif op.milestone:
                ins.then_inc(op.tok[0], 1)


class Rot:
    def __init__(self, items):
        self.items = items
        self.i = 0

    def next(self):
        it = self.items[self.i % len(self.items)]
        self.i += 1
        return it


def build_program():
    nc = bass.Bass("TRN2", target_bir_lowering=False)
    nc.dge_precook = False
    P = Plan()

    def din(name, shape, dt=F32):
        return nc.dram_tensor(name, list(shape), dt, kind="ExternalInput").ap()

    def dout(name, shape):
        return nc.dram_tensor(name, list(shape), F32, kind="ExternalOutput").ap()

    xp = din("xp", [SEQ, D])
    xs = din("xs", [NS, D])
    sc = din("sc", [NS * 2, D])
    gn_d = din("gn", [128, 8])
    cw_d = din("cw", [128, 8 * 3])
    wt_d = din("wt", [18, 128, 8 * 512], MMDT)
    wsT_d = din("wsT", [128, 8 * 128])
    bs_d = din("bs", [8 * 128])
    w00_d = din("w00", [8])
    b0_d = din("b0", [8])
    gv_d = din("gv", [D])
    gf_d = din("gf", [D])
    yp = dout("yp", [SEQ, D])
    ys = dout("ys", [NS, D])
    scp = dout("scp", [2, D])
    scs = dout("scs", [NS, 2 * D])
    svs = dout("svs", [NS, D])

    es = ExitStack()
    with es:
        def sb(name, shape, dt=F32):
            return es.enter_context(nc.sbuf_tensor(name, list(shape), dt))

        ring = [sb(f"ring{i}", [128, 8, 512], MMDT) for i in range(RING)]
        hT = sb("hT", [128, 8, NT + NS], MMDT)
        hTs = hT[:, :, NT:NT + NS]
        vn = sb("vn", [128, 4, D], MMDT)
        vs32 = sb("vs32", [NS, D], F32)
        mixT = sb("mixT", [128, 16, NT], MMDT)
        mixTs = sb("mixTs", [128, 16, NS], MMDT)
        tok = [sb(f"tok{i}", [128, D], F32) for i in range(5)]
        xin = [sb(f"xin{i}", [128, D], F32) for i in range(3)]
        junk = sb("junk", [128, D], BF16)
        fa_all = sb("fa_all", [128, 2, NT], F32)
        fa = [fa_all[:, i, :] for i in range(2)]
        fb = [sb(f"fb{i}", [128, NT + 2], F32) for i in range(2)]
        fc = [sb(f"fc{i}", [128, NT], F32) for i in range(2)]
        fd = [sb(f"fd{i}", [128, NT], F32) for i in range(2)]
        sa = [sb(f"sa{i}", [128, NS], F32) for i in range(2)]
        sbb = [sb(f"sbb{i}", [128, NS], F32) for i in range(2)]
        scc = [sb(f"scc{i}", [128, NS], F32) for i in range(2)]
        sd = [sb(f"sd{i}", [128, NS], F32) for i in range(2)]
        hbs = sb("hbs", [128, 8, NS], F32)
        stT = sb("stT", [128, 8, 2 * NS], F32)
        carry = sb("carry", [128, 8, 2], F32)
        gvbc = sb("gvbc", [128, D], F32)
        gfbc = sb("gfbc", [128, D], F32)
        WT = sb("WT", [128, 8, 128], MMDT)
        bbc = sb("bbc", [128, 8, 128], F32)
        onesr = sb("onesr", [1, 128], MMDT)
        ones32 = sb("ones32", [128, 128], F32)
        ident = sb("ident", [128, 128], F32)
        gn = sb("gnt", [128, 8], F32)
        cw = sb("cwt", [128, 8 * 3], F32)
        w00 = sb("w00t", [128, 8], F32)
        b0 = sb("b0t", [128, 8], F32)
        mhalf = sb("mhalf", [128, 1], F32)
        stat = sb("stat", [128, 18], F32)
        ps = es.enter_context(nc.psum_tensor("ps", [128, 8, 512], F32))

        sem = lambda n: es.enter_context(nc.semaphore(n))
        tick = {e: sem(f"tick_{e}") for e in ENGS if e != "sync"}
        s_ring = [sem(f"s_ring{i}") for i in range(RING)]
        s_const = sem("s_const")
        s_const2 = sem("s_const2")
        s_xin = [sem(f"s_xin{i}") for i in range(3)]
        s_tokld = [sem(f"s_tokld{i}") for i in range(5)]
        s_tokst = [sem(f"s_tokst{i}") for i in range(5)]
        s_out = sem("s_out")
        block = es.enter_context(nc.Block())

        B = Buf
        b_ring = [[B(f"ring{i}q{q}") for q in range(4)] for i in range(RING)]
        b_hT = [B(f"hT{t}") for t in range(4)]
        b_hTs = B("hTs")
        b_vn = [B(f"vn{t}") for t in range(4)]
        b_vs32 = B("vs32")
        b_mixT = [B(f"mixT{k}") for k in range(16)]
        b_mixTs = [B(f"mixTs{k}") for k in range(16)]
        b_tok = [B(f"tok{i}") for i in range(5)]
        b_xin = [B(f"xin{i}") for i in range(3)]
        b_junk = B("junk")
        b_fa = [B(f"fa{i}") for i in range(2)]
        b_fb = [B(f"fb{i}") for i in range(2)]
        b_fc = [B(f"fc{i}") for i in range(2)]
        b_fd = [B(f"fd{i}") for i in range(2)]
        b_sa = [B(f"sa{i}") for i in range(2)]
        b_sb = [B(f"sbb{i}") for i in range(2)]
        b_sc = [B(f"scc{i}") for i in range(2)]
        b_sd = [B(f"sd{i}") for i in range(2)]
        b_hbs = [B(f"hbs{j}") for j in range(8)]
        b_stT = B("stT")
        b_carry = [B(f"carry{j}") for j in range(8)]
        b_const = B("const")
        b_ps = [B(f"ps{i}") for i in range(8)]
        b_stat = [B(f"stat{i}") for i in range(8)]

        rF = Rot([0, 1, 2, 7])
        rT = Rot([4, 5, 6])
        BANK_S = 7
        BANK_M = 3
        r_fa, r_fb, r_fc, r_fd = Rot([0, 1]), Rot([0, 1]), Rot([0, 1]), Rot([0, 1])
        r_sa, r_sb, r_sc, r_sd = Rot([0, 1]), Rot([0, 1]), Rot([0, 1]), Rot([0, 1])
        r_xin = Rot([0, 1, 2])
        r_stat = Rot(list(range(8)))
        r_tokgv = Rot([0, 1])

        b_cd = {n: B("cd_" + n) for n in ("gn", "cw", "w00", "b0", "gv", "gf", "wsT", "bs")}
        c_ops = []

        c_ops2 = []

        def cdma(name, out, in_, late=False):
            (c_ops2 if late else c_ops).append(
                P.add("sync", lambda e, o=out, i_=in_: e.dma_start(out=o, in_=i_),
                      writes=[b_cd[name]], dma_sem=(s_const2 if late else s_const)))

        def const_dmas_early():
            cdma("gn", gn[:], gn_d)
            cdma("gv", gvbc[:], gv_d.partition_broadcast(128))

        def const_dmas_late():
            cdma("cw", cw[:], cw_d, late=True)
            cdma("w00", w00[:], w00_d.partition_broadcast(128), late=True)
            cdma("b0", b0[:], b0_d.partition_broadcast(128), late=True)
            cdma("gf", gfbc[:], gf_d.partition_broadcast(128), late=True)
            cdma("wsT", tok[4][:], wsT_d, late=True)
            cdma("bs", bbc[:].rearrange("p h t -> p (h t)"), bs_d.partition_broadcast(128), late=True)

        b_setup = B("setup")
        b_ident, b_WT = B("ident"), B("WT")

        def const_compute():
            P.add("gpsimd", lambda e: e.affine_select(out=WT[:].rearrange("p h t -> p (h t)"), in_=tok[4][:],
                                                      pattern=[[0, 8], [1, 128]], compare_op=ALU.is_ge,
                                                      fill=0.0, base=0, channel_multiplier=-1),
                  reads=[b_cd["wsT"]], writes=[b_WT, b_tok[4]])
            P.add("vector", lambda e: e.tensor_copy(out=onesr[:], in_=ones32[0:1, :]),
                  reads=[b_setup, b_ident, b_WT] + list(b_cd.values()), writes=[b_const])

        P.add("gpsimd", lambda e: e.memset(mhalf[:], -0.5), writes=[b_setup])
        P.add("gpsimd", lambda e: e.memset(ones32[:], 1.0), writes=[b_setup])
        P.add("gpsimd", lambda e: e.memset(carry[:], 0.0), writes=b_carry)
        P.add("scalar", lambda e: e.activation(out=stat[:, 16:17], in_=mhalf[:, 0:1], func=AF.Gelu),
              reads=[b_setup], writes=[B("warm")])
        P.add("gpsimd", lambda e: e.affine_select(out=ident[:], in_=ones32[:], pattern=[[1, 128]],
                                                  compare_op=ALU.is_equal, fill=0.0, base=0,
                                                  channel_multiplier=-1),
              reads=[b_setup], writes=[b_ident])

        wtiles = []
        for _b in range(NBLK):
            wtiles += [("v", 0), ("v", 1)]
            wtiles += [("B", j) for j in range(8)]
            wtiles += [("A", i) for i in range(4)]
            wtiles += [("o", 0, 0), ("o", 0, 1), ("o", 1, 0), ("o", 1, 1)]
        n_w = len(wtiles)
        w_issued = [0]

        def issue_weight(idx):
            slot = idx % RING
            rt = ring[slot]
            P.add("sync", lambda e, rt=rt, idx=idx: e.dma_start(out=rt[:].rearrange("p k c -> p (k c)"),
                                                               in_=wt_d[idx % 18]),
                  writes=b_ring[slot], dma_sem=s_ring[slot])

        def prefetch(upto):
            while w_issued[0] < min(upto, n_w):
                issue_weight(w_issued[0])
                w_issued[0] += 1

        w_cursor = [0]

        def next_weight(expect):
            idx = w_cursor[0]
            assert wtiles[idx][0] == expect, (wtiles[idx], expect)
            assert idx < w_issued[0]
            w_cursor[0] += 1
            return idx % RING

        def release_weights():
            prefetch(w_cursor[0] + RING)

        class Group:
            pass

        def make_prompt_group(b):
            g = Group()
            g.kind = "P"
            g.n = NT
            g.tiles = [(t * 128, 128) for t in range(4)]
            g.hT = hT
            g.b_hT = b_hT
            g.mixT = mixT
            g.b_mixT = b_mixT
            g.x_rows = lambda t: xp[b * NT + t * 128:b * NT + (t + 1) * 128, :]
            g.y_rows = lambda t: yp[b * NT + t * 128:b * NT + (t + 1) * 128, :]
            g.tokidx = [0, 1, 2, 3]
            g.blk = b
            return g

        def make_sample_group():
            g = Group()
            g.kind = "S"
            g.n = NS
            g.tiles = [(0, NS)]
            g.hT = hTs
            g.b_hT = [b_hTs]
            g.mixT = mixTs
            g.b_mixT = b_mixTs
            g.x_rows = lambda t: xs[:, :]
            g.y_rows = lambda t: ys[:, :]
            g.tokidx = [4]
            g.blk = 0
            return g

        def rms_inv(src_ap, nr, b_src, ncols):
            si = r_stat.next()
            ssq = stat[:nr, 2 * si:2 * si + 1]
            rinv = stat[:nr, 2 * si + 1:2 * si + 2]
            bst = b_stat[si]
            P.add("scalar", lambda e: e.activation(out=junk[:nr, 0:ncols], in_=src_ap, func=AF.Square,
                                                   accum_out=ssq),
                  reads=(b_src if isinstance(b_src, list) else [b_src]), writes=[b_junk, bst])
            P.add("gpsimd", lambda e: e.tensor_scalar(out=ssq, in0=ssq, scalar1=1.0 / ncols, scalar2=EPS,
                                                      op0=ALU.mult, op1=ALU.add),
                  reads=[bst], writes=[bst])
            P.add("gpsimd", lambda e: e.tensor_tensor(out=rinv, in0=ssq, in1=mhalf[:nr, :], op=ALU.pow),
                  reads=[bst, b_setup], writes=[bst])
            return rinv, bst

        class XSrc:
            def __init__(self, t, b, sm):
                self.t, self.s = t, sm
                self.b = list(b) if isinstance(b, (list, tuple)) else [b]

        xsrc_xin = [XSrc(xin[i], b_xin[i], s_xin[i]) for i in range(3)]
        xsrc_tok = [XSrc(tok[i], b_tok[i], s_tokld[i]) for i in range(5)]
        s_vnld = sem("s_vnld")
        xsrc_vn3 = XSrc(fa_all[:].rearrange("p a n -> p (a n)"), [b_fa[0], b_fa[1]], s_vnld)

        def xprep_load(g, t, src=None, queue="sync"):
            xs_ = xsrc_xin[r_xin.next()] if src is None else src
            P.add(queue, lambda e: e.dma_start(out=xs_.t[:g.tiles[t][1], :], in_=g.x_rows(t)),
                  writes=xs_.b, dma_sem=xs_.s)
            return xs_

        pre_state = {}

        def xprep_pre_a(g, t, xi):
            c0, nr = g.tiles[t]
            xa = xi.t
            pre_state[(id(g), t)] = rms_inv(xa[:nr, :], nr, xi.b, D)

        def xprep_pre_b(g, t, xi):
            c0, nr = g.tiles[t]
            xa = xi.t
            rinv, bst = pre_state.pop((id(g), t))
            P.add("scalar", lambda e: e.activation(out=xa[:nr, :], in_=xa[:nr, :], func=AF.Copy, scale=rinv),
                  reads=[bst] + xi.b, writes=xi.b)

        def xprep_pre(g, t, xi):
            xprep_pre_a(g, t, xi)
            xprep_pre_b(g, t, xi)

        def xprep_post(g, t, xi):
            c0, nr = g.tiles[t]
            xa = xi.t
            for hf in range(2):
                bank = rF.next()
                for q in range(4):
                    k = hf * 4 + q
                    P.add("tensor", lambda e, k=k, q=q, bank=bank: e.transpose(
                        out=ps[:, bank, q * 128:q * 128 + nr], in_=xa[:nr, k * 128:(k + 1) * 128],
                        identity=ident[:nr, :nr]),
                        reads=xi.b + [b_ident], writes=[b_ps[bank]])
                P.add("vector", lambda e, hf=hf, bank=bank: e.tensor_tensor(
                    out=g.hT[:, hf * 4:hf * 4 + 4, c0:c0 + nr],
                    in0=ps[:, bank, :].rearrange("p (q t) -> p q t", q=4)[:, :, 0:nr],
                    in1=gn[:, hf * 4:hf * 4 + 4].unsqueeze(2).to_broadcast([128, 4, nr]), op=ALU.mult),
                    reads=[b_ps[bank], b_cd["gn"]], writes=[g.b_hT[t]])

        def xprep_compute(g, t, xi):
            xprep_pre(g, t, xi)
            xprep_post(g, t, xi)

        def load_state_dma():
            P.add("sync", lambda e: e.dma_start(out=tok[2][:2 * NS, :], in_=sc),
                  writes=[b_tok[2]], dma_sem=s_tokld[2])
            sc3 = sc.rearrange("(s r) d -> s r d", r=2)
            P.add("sync", lambda e: e.dma_start(out=scs[:, 0:D], in_=sc3[:, 1, :]), dma_sem=s_out)

        def load_state():
            for hf in range(2):
                bank = rF.next()
                for q in range(4):
                    k = hf * 4 + q
                    P.add("tensor", lambda e, k=k, q=q, bank=bank: e.transpose(
                        out=ps[:, bank, q * 128:q * 128 + 2 * NS], in_=tok[2][:2 * NS, k * 128:(k + 1) * 128],
                        identity=ident[:2 * NS, :2 * NS]),
                        reads=[b_tok[2], b_ident], writes=[b_ps[bank]])
                for q in range(4):
                    k = hf * 4 + q
                    P.add("vector", lambda e, k=k, q=q, bank=bank: e.tensor_copy(
                        out=stT[:, k, :], in_=ps[:, bank, q * 128:q * 128 + 2 * NS]),
                        reads=[b_ps[bank]], writes=[b_stT])

        def v_tile(g, t, slots):
            c0, nr = g.tiles[t]
            gi = r_tokgv.next()
            gvt = tok[gi]
            for half in range(2):
                bank = rT.next()
                rt = ring[slots[half]]
                for k in range(8):
                    P.add("tensor", lambda e, k=k, bank=bank, rt=rt: e.matmul(
                        ps[:nr, bank, :], lhsT=g.hT[:, k, c0:c0 + nr], rhs=rt[:, k, :],
                        start=(k == 0), stop=(k == 7)),
                        reads=[g.b_hT[t]] + b_ring[slots[half]], writes=[b_ps[bank]])
                P.add("scalar", lambda e, half=half, bank=bank: e.activation(
                    out=gvt[:nr, half * 512:(half + 1) * 512], in_=ps[:nr, bank, :], func=AF.Gelu),
                    reads=[b_ps[bank]], writes=[b_tok[gi]])
            rinv, bst = rms_inv(gvt[:nr, :], nr, b_tok[gi], D)
            if g.kind == "P":
                P.add("vector", lambda e: e.scalar_tensor_tensor(
                    out=vn[:nr, t, :], in0=gvt[:nr, :], scalar=rinv, in1=gvbc[:nr, :],
                    op0=ALU.mult, op1=ALU.mult),
                    reads=[b_tok[gi], bst, b_cd["gv"]], writes=[b_vn[t]])
            else:
                P.add("vector", lambda e: e.scalar_tensor_tensor(
                    out=vs32[:nr, :], in0=gvt[:nr, :], scalar=rinv, in1=gvbc[:nr, :],
                    op0=ALU.mult, op1=ALU.mult),
                    reads=[b_tok[gi], bst, b_cd["gv"]], writes=[b_vs32])
                P.add("sync", lambda e: e.dma_start(out=svs, in_=vs32[:, :]),
                      reads=[b_vs32], dma_sem=s_out)

        def v_phase(groups, slots):
            for g in groups:
                for t in range(len(g.tiles)):
                    v_tile(g, t, slots)

        def feat_mm(g, slot, q, col0):
            bank = rF.next()
            rt = ring[slot]
            for k in range(8):
                P.add("tensor", lambda e, k=k: e.matmul(
                    ps[:, bank, 0:g.n], lhsT=rt[:, k, col0:col0 + 128], rhs=g.hT[:, k, 0:g.n],
                    start=(k == 0), stop=(k == 7)),
                    reads=g.b_hT + [b_ring[slot][q]], writes=[b_ps[bank]])
            return bank

        with_sample = [False]
        samp_piece = {}

        def feat_mm2(slot, q, col0):
            bx, by = rF.next(), rF.next()
            rt = ring[slot]
            for k in range(8):
                P.add("tensor", lambda e, k=k: e.matmul(
                    ps[:, bx, 0:256], lhsT=rt[:, k, col0:col0 + 128], rhs=hT[:, k, 0:256],
                    start=(k == 0), stop=(k == 7)),
                    reads=[b_hT[0], b_hT[1], b_ring[slot][q]], writes=[b_ps[bx]])
                P.add("tensor", lambda e, k=k: e.matmul(
                    ps[:, by, 0:256 + NS], lhsT=rt[:, k, col0:col0 + 128], rhs=hT[:, k, 256:NT + NS],
                    start=(k == 0), stop=(k == 7)),
                    reads=[b_hT[2], b_hT[3], b_hTs, b_ring[slot][q]], writes=[b_ps[by]])
            return bx, by

        def seg_src(g, slot, q, col0):
            if g.kind == "P":
                if with_sample[0]:
                    bx, by = feat_mm2(slot, q, col0)
                    samp_piece[q] = (ps[:, by, 256:256 + NS], b_ps[by])
                    return [(ps[:, bx, 0:256], b_ps[bx], 0, 256), (ps[:, by, 0:256], b_ps[by], 256, 512)]
                bank = feat_mm(g, slot, q, col0)
                return [(ps[:, bank, 0:g.n], b_ps[bank], 0, g.n)]
            sap, sbuf = samp_piece.pop(q)
            return [(sap, sbuf, 0, NS)]

        def interleave(gens):
            gens = list(gens)
            while gens:
                for gen in list(gens):
                    try:
                        next(gen)
                    except StopIteration:
                        gens.remove(gen)

        def b_side(groups, slot, j):
            interleave([b_side_g(g, slot, j) for g in groups])

        def b_side_g(g, slot, j):
            n = g.n
            isP = g.kind == "P"
            ia = (r_fa if isP else r_sa).next()
            ib = (r_fb if isP else r_sb).next()
            ic = (r_fc if isP else r_sc).next()
            idd = (r_fd if isP else r_sd).next()
            ta, b_ta = (fa[ia], b_fa[ia]) if isP else (sa[ia], b_sa[ia])
            tc, b_tc = (fc[ic], b_fc[ic]) if isP else (scc[ic], b_sc[ic])
            td, b_td = (fd[idd], b_fd[idd]) if isP else (sd[idd], b_sd[idd])
            w0, w1, w2 = (cw[:, 3 * j + i:3 * j + i + 1] for i in range(3))
            for sap, sbuf, c0, c1 in seg_src(g, slot, 0, 0):
                P.add("scalar", lambda e, sap=sap, c0=c0, c1=c1: e.activation(out=ta[:, c0:c1], in_=sap, func=AF.Copy),
                      reads=[sbuf], writes=[b_ta])
            yield
            pieces = seg_src(g, slot, 2, 256)
            if isP:
                tb, b_tb = fb[ib], b_fb[ib]
                P.add("vector", lambda e: e.tensor_copy(out=tb[:, 0:2], in_=carry[:, j, :]),
                      reads=[b_carry[j]], writes=[b_tb])
                for sap, sbuf, c0, c1 in pieces:
                    P.add("vector", lambda e, sap=sap, c0=c0, c1=c1: e.tensor_tensor(
                        out=tb[:, 2 + c0:2 + c1], in0=sap, in1=ta[:, c0:c1], op=ALU.mult),
                        reads=[sbuf, b_ta], writes=[b_tb])
                P.add("vector", lambda e: e.tensor_copy(out=carry[:, j, :], in_=tb[:, n:n + 2]),
                      reads=[b_tb], writes=[b_carry[j]])
                h0, h1, h2 = tb[:, 0:n], tb[:, 1:n + 1], tb[:, 2:n + 2]
                rd = [b_tb]
            else:
                for sap, sbuf, c0, c1 in pieces:
                    P.add("vector", lambda e, sap=sap: e.tensor_tensor(out=hbs[:, j, :], in0=sap,
                                                                     in1=ta[:, 0:n], op=ALU.mult),
                          reads=[sbuf, b_ta], writes=[b_hbs[j]])
                st3 = stT[:, j, :].rearrange("p (s r) -> p s r", r=2)
                h0, h1, h2 = st3[:, :, 0], st3[:, :, 1], hbs[:, j, :]
                rd = [b_hbs[j], b_stT]
            P.add("vector", lambda e: e.tensor_scalar(out=tc[:, 0:n], in0=h0, scalar1=w0, scalar2=None,
                                                      op0=ALU.mult),
                  reads=rd + [b_const], writes=[b_tc])
            P.add("vector", lambda e: e.scalar_tensor_tensor(out=tc[:, 0:n], in0=h1, scalar=w1,
                                                             in1=tc[:, 0:n], op0=ALU.mult, op1=ALU.add),
                  reads=rd + [b_const, b_tc], writes=[b_tc])
            P.add("vector", lambda e: e.scalar_tensor_tensor(out=tc[:, 0:n], in0=h2, scalar=w2,
                                                             in1=tc[:, 0:n], op0=ALU.mult, op1=ALU.add),
                  reads=rd + [b_const, b_tc], writes=[b_tc])
            yield
            for sap, sbuf, c0, c1 in seg_src(g, slot, 3, 384):
                P.add("scalar", lambda e, sap=sap, c0=c0, c1=c1: e.activation(out=td[:, c0:c1], in_=sap, func=AF.Tanh,
                                                                              scale=0.5),
                      reads=[sbuf], writes=[b_td])
                P.add("vector", lambda e, sap=sap, c0=c0, c1=c1: e.scalar_tensor_tensor(
                    out=td[:, c0:c1], in0=td[:, c0:c1], scalar=1.0, in1=sap, op0=ALU.add, op1=ALU.mult),
                    reads=[sbuf, b_td], writes=[b_td])
            yield
            for sap, sbuf, c0, c1 in seg_src(g, slot, 1, 128):
                P.add("vector", lambda e, sap=sap, c0=c0, c1=c1: e.tensor_tensor(
                    out=tc[:, c0:c1], in0=sap, in1=tc[:, c0:c1], op=ALU.mult),
                    reads=[sbuf, b_tc], writes=[b_tc])
            P.add("vector", lambda e: e.scalar_tensor_tensor(out=g.mixT[:, 8 + j, 0:n], in0=tc[:, 0:n],
                                                             scalar=0.5, in1=td[:, 0:n], op0=ALU.mult,
                                                             op1=ALU.mult),
                  reads=[b_tc, b_td], writes=[g.b_mixT[8 + j]])

        def a_side(groups, slot, h):
            interleave([a_side_g(g, slot, h) for g in groups])

        def a_side_g(g, slot, h):
            hl = h % 2
            n = g.n
            isP = g.kind == "P"
            ia = (r_fa if isP else r_sa).next()
            idd = (r_fd if isP else r_sd).next()
            ta, b_ta = (fa[ia], b_fa[ia]) if isP else (sa[ia], b_sa[ia])
            td, b_td = (fd[idd], b_fd[idd]) if isP else (sd[idd], b_sd[idd])
            for sap, sbuf, c0, c1 in seg_src(g, slot, hl, hl * 128):
                P.add("scalar", lambda e, sap=sap, c0=c0, c1=c1: e.activation(out=ta[:, c0:c1], in_=sap, func=AF.Gelu),
                      reads=[sbuf], writes=[b_ta])
            yield
            for sap, sbuf, c0, c1 in seg_src(g, slot, 2 + hl, 256 + hl * 128):
                P.add("scalar", lambda e, sap=sap, c0=c0, c1=c1: e.activation(out=td[:, c0:c1], in_=sap, func=AF.Tanh,
                                                                              scale=0.5),
                      reads=[sbuf], writes=[b_td])
                P.add("vector", lambda e, sap=sap, c0=c0, c1=c1: e.scalar_tensor_tensor(
                    out=td[:, c0:c1], in0=td[:, c0:c1], scalar=1.0, in1=sap, op0=ALU.add, op1=ALU.mult),
                    reads=[sbuf, b_td], writes=[b_td])
            yield
            if isP:
                for c in range(4):
                    P.add("tensor", lambda e, c=c: e.matmul(
                        ps[:, BANK_M, c * 128:(c + 1) * 128], lhsT=vn[:, c, h * 128:(h + 1) * 128],
                        rhs=WT[:, h, :], start=True, stop=True),
                        reads=[b_vn[c], b_const], writes=[b_ps[BANK_M]])
                ic = r_fc.next()
                tc, b_tc = fc[ic], b_fc[ic]
                P.add("vector", lambda e: e.tensor_tensor(
                    out=tc[:, 0:n].rearrange("p (c t) -> p c t", c=4),
                    in0=ps[:, BANK_M, 0:n].rearrange("p (c t) -> p c t", c=4),
                    in1=bbc[:, h, :].unsqueeze(1).to_broadcast([128, 4, 128]), op=ALU.add),
                    reads=[b_ps[BANK_M], b_const], writes=[b_tc])
                P.add("vector", lambda e: e.tensor_tensor(out=ta[:, 0:n], in0=tc[:, 0:n],
                                                          in1=ta[:, 0:n], op=ALU.mult),
                      reads=[b_tc, b_ta], writes=[b_ta])
            else:
                cbank = rT.next()
                cmap = ps[:, cbank, 0:NS]
                P.add("tensor", lambda e: e.transpose(
                    out=cmap, in_=vs32[:n, h * 128:(h + 1) * 128], identity=ident[:n, :n]),
                    reads=[b_vs32, b_const], writes=[b_ps[cbank]])
                ic = r_sc.next()
                tc, b_tc = scc[ic], b_sc[ic]
                P.add("vector", lambda e: e.tensor_scalar(out=tc[:, 0:n], in0=cmap,
                                                          scalar1=w00[:, h:h + 1], scalar2=b0[:, h:h + 1],
                                                          op0=ALU.mult, op1=ALU.add),
                      reads=[b_ps[cbank], b_const], writes=[b_tc])
                P.add("vector", lambda e: e.tensor_tensor(out=ta[:, 0:n], in0=tc[:, 0:n], in1=ta[:, 0:n],
                                                          op=ALU.mult),
                      reads=[b_tc, b_ta], writes=[b_ta])
            P.add("vector", lambda e: e.scalar_tensor_tensor(out=g.mixT[:, h, 0:n], in0=ta[:, 0:n], scalar=0.5,
                                                             in1=td[:, 0:n], op0=ALU.mult, op1=ALU.mult),
                  reads=[b_ta, b_td], writes=[g.b_mixT[h]])

        def load_resid(groups):
            for g in groups:
                for t, (c0, nr) in enumerate(g.tiles):
                    ti = g.tokidx[t]
                    P.add("sync", lambda e, t=t, ti=ti, nr=nr, g=g: e.dma_start(out=tok[ti][:nr, :], in_=g.x_rows(t)),
                          writes=[b_tok[ti]], dma_sem=s_tokld[ti])

        def out_tile(g, t, half, slots):
            c0, nr = g.tiles[t]
            ti = g.tokidx[t]
            bank = rT.next()
            korder = list(range(8, 16)) + list(range(8))
            for i_, k in enumerate(korder):
                rt = ring[slots[k // 8]]
                P.add("tensor", lambda e, k=k, rt=rt, i_=i_: e.matmul(
                    ps[:nr, bank, :], lhsT=g.mixT[:, k, c0:c0 + nr], rhs=rt[:, k % 8, :],
                    start=(i_ == 0), stop=(i_ == 15)),
                    reads=[g.b_mixT[k]] + b_ring[slots[k // 8]], writes=[b_ps[bank]])
            yt = tok[ti]
            P.add("vector", lambda e: e.tensor_tensor(
                out=yt[:nr, half * 512:(half + 1) * 512], in0=ps[:nr, bank, :],
                in1=yt[:nr, half * 512:(half + 1) * 512], op=ALU.add),
                reads=[b_ps[bank], b_tok[ti]], writes=[b_tok[ti]])
            if half == 1:
                rinv, bst = rms_inv(yt[:nr, :], nr, b_tok[ti], D)

                def finish():
                    P.add("vector", lambda e: e.scalar_tensor_tensor(
                        out=yt[:nr, :], in0=yt[:nr, :], scalar=rinv, in1=gfbc[:nr, :],
                        op0=ALU.mult, op1=ALU.mult),
                        reads=[b_tok[ti], bst, b_const], writes=[b_tok[ti]])
                    P.add("sync", lambda e: e.dma_start(out=g.y_rows(t), in_=yt[:nr, :]),
                          reads=[b_tok[ti]], dma_sem=s_tokst[ti])
                return finish
            return None

        def out_proj_half(groups, half, slots, between=None):
            pending = None
            for g in groups:
                for t in range(len(g.tiles)):
                    fin = out_tile(g, t, half, slots)
                    if pending is not None:
                        pending()
                    pending = fin
                    if between is not None:
                        between(g, t)
            if pending is not None:
                pending()

        def emit_state_outputs_sample():
            banks = [rT.next(), rT.next()]
            for k in range(8):
                bank = banks[k // 4]
                q = k % 4
                P.add("tensor", lambda e, k=k, q=q, bank=bank: e.transpose(
                    out=ps[:NS, bank, q * 128:(q + 1) * 128], in_=hbs[:, k, :], identity=ident[:, :]),
                    reads=[b_hbs[k], b_const], writes=[b_ps[bank]])
            for hf in range(2):
                P.add("scalar", lambda e, hf=hf: e.activation(out=tok[4][:NS, hf * 512:(hf + 1) * 512],
                                                              in_=ps[:NS, banks[hf], :], func=AF.Copy),
                      reads=[b_ps[banks[hf]]], writes=[b_tok[4]])
            P.add("sync", lambda e: e.dma_start(out=scs[:, D:2 * D], in_=tok[4][:NS, :]),
                  reads=[b_tok[4]], dma_sem=s_tokst[4])

        def emit_state_outputs_prompt():
            banks = [rT.next(), rT.next()]
            for k in range(8):
                bank = banks[k // 4]
                q = k % 4
                P.add("tensor", lambda e, k=k, q=q, bank=bank: e.transpose(
                    out=ps[:2, bank, q * 128:(q + 1) * 128], in_=carry[:, k, :], identity=ident[:, :]),
                    reads=[b_carry[k], b_const], writes=[b_ps[bank]])
            for hf in range(2):
                P.add("scalar", lambda e, hf=hf: e.activation(out=tok[4][:2, hf * 512:(hf + 1) * 512],
                                                              in_=ps[:2, banks[hf], :], func=AF.Copy),
                      reads=[b_ps[banks[hf]]], writes=[b_tok[4]])
            P.add("sync", lambda e: e.dma_start(out=scp, in_=tok[4][:2, :]),
                  reads=[b_tok[4]], dma_sem=s_tokst[4])

        gS = make_sample_group()
        g0 = make_prompt_group(0)
        const_dmas_early()
        xi_s = xprep_load(gS, 0, xsrc_tok[3])
        pend = [xprep_load(g0, 0), xprep_load(g0, 1), xprep_load(g0, 2, xsrc_tok[0]), xprep_load(g0, 3, xsrc_tok[1])]
        load_state_dma()
        prefetch(2)
        const_dmas_late()
        P.unify(c_ops)
        P.unify(c_ops2)
        prefetch(RING)
        xprep_pre_a(gS, 0, xi_s)
        xprep_pre_a(g0, 0, pend[0])
        xprep_pre_b(gS, 0, xi_s)
        xprep_pre_a(g0, 1, pend[1])
        xprep_pre_b(g0, 0, pend[0])
        xprep_post(gS, 0, xi_s)
        for t in range(4):
            if t + 2 < 4:
                xprep_pre_a(g0, t + 2, pend[t + 2])
            if t + 1 < 4:
                xprep_pre_b(g0, t + 1, pend[t + 1])
            xprep_post(g0, t, pend[t])
        const_compute()

        for b in range(NBLK):
            gP = make_prompt_group(b)
            groups = [gP, gS] if b == 0 else [gP]
            with_sample[0] = (b == 0)
            s0 = next_weight("v")
            s1 = next_weight("v")
            v_phase(groups, [s0, s1])
            release_weights()
            if b == 0:
                load_state()
            for j in range(8):
                s = next_weight("B")
                b_side(groups, s, j)
                release_weights()
                if j == 3 and b + 1 < NBLK:
                    gN = make_prompt_group(b + 1)
                    pend = [xprep_load(gN, 0), xprep_load(gN, 1), xprep_load(gN, 2)]
                if j == 5:
                    load_resid([gP])
            if b == 0:
                emit_state_outputs_sample()
                load_resid([gS])
            if b == NBLK - 1:
                emit_state_outputs_prompt()
            for i in range(4):
                s = next_weight("A")
                a_side(groups, s, 2 * i)
                a_side(groups, s, 2 * i + 1)
                release_weights()
                if b + 1 < NBLK:
                    if i == 0:
                        for t_ in range(len(pend)):
                            xprep_pre_a(gN, t_, pend[t_])
                    else:
                        xprep_pre_b(gN, i - 1, pend[i - 1])
            if b + 1 < NBLK:
                pend.append(xprep_load(gN, 3, xsrc_vn3))
                xprep_pre_a(gN, 3, pend[3])
            so = [next_weight("o"), next_weight("o")]
            out_proj_half(groups, 0, so)
            release_weights()
            if b + 1 < NBLK:
                xprep_pre_b(gN, 3, pend[3])
            so = [next_weight("o"), next_weight("o")]
            if b + 1 < NBLK:
                def between(g, t, gN=gN, pend=pend):
                    if g.kind != "P":
                        return
                    xprep_post(gN, t, pend[t])
                out_proj_half(groups, 1, so, between)
            else:
                out_proj_half(groups, 1, so)
            release_weights()

        P.finalize(tick)
        final_waits = [(s_out, P.dma_count.get(id(s_out), 0))]
        for s in s_tokst + s_xin:
            final_waits.append((s, P.dma_count.get(id(s), 0)))

        @block.sync
        def _(eng):
            P.emit("sync", eng)
            for s, v in final_waits:
                if v > 0:
                    eng.wait_ge(s, v)

        @block.scalar
        def _(eng):
            P.emit("scalar", eng)

        @block.vector
        def _(eng):
            P.emit("vector", eng)

        @block.gpsimd
        def _(eng):
            P.emit("gpsimd", eng)

        @block.tensor
        def _(eng):
            P.emit("tensor", eng)

    return nc


_NC_CACHE = {}


def kernel(x_prompt, x_sample, state_conv, g_norm, w_in, w_s, b_s, g_v, conv_w, w_out, g_final):
    f = lambda a: np.ascontiguousarray(np.asarray(a, dtype=np.float32))
    x_prompt, x_sample, state_conv = f(x_prompt), f(x_sample), f(state_conv)
    if "nc" not in _NC_CACHE:
        _NC_CACHE["nc"] = build_program()
    nc = _NC_CACHE["nc"]
    gn = f(np.asarray(g_norm)[0].reshape(8, 128).T)
    cw = f(np.asarray(conv_w)[0].reshape(3, 8, 128).transpose(2, 1, 0).reshape(128, 24))
    wsT = f(np.asarray(w_s)[0].transpose(2, 0, 1).reshape(128, 8 * 128))
    bs = f(np.asarray(b_s)[0].reshape(8 * 128))
    w00 = f(np.asarray(w_s)[0, :, 0, 0])
    b0 = f(np.asarray(b_s)[0, :, 0])
    wi = np.asarray(w_in, dtype=np.float32)[0]
    wo = np.asarray(w_out, dtype=np.float32)[0]
    tiles = []
    for half in range(2):
        tiles.append(wi[:, D + half * 512:D + (half + 1) * 512])
    for j in range(8):
        tiles.append(np.concatenate([wi[:, 3 * D + sg * D + j * 128:3 * D + sg * D + (j + 1) * 128] for sg in range(4)], axis=1))
    for i in range(4):
        tiles.append(np.concatenate([wi[:, i * 256:(i + 1) * 256], wi[:, 2 * D + i * 256:2 * D + (i + 1) * 256]], axis=1))
    for half in range(2):
        for kh in range(2):
            tiles.append(wo[kh * D:(kh + 1) * D, half * 512:(half + 1) * 512])
    wt = np.ascontiguousarray(np.stack([t_.reshape(8, 128, 512).transpose(1, 0, 2).reshape(128, 8 * 512) for t_ in tiles], axis=0))
    shared = {
        "gn": gn, "cw": cw, "wt": wt,
        "wsT": wsT, "bs": bs, "w00": w00, "b0": b0, "gv": f(np.asarray(g_v)[0]), "gf": f(g_final),
    }
    in_maps = []
    for c in range(N_CORES):
        m = dict(shared)
        m["xp"] = x_prompt[c]
        m["xs"] = f(x_sample[c * NS:(c + 1) * NS, 0, :])
        m["sc"] = f(state_conv[0, c * NS:(c + 1) * NS].reshape(2 * NS, D))
        in_maps.append(m)
    res = run_bass_kernel_spmd(nc, in_maps, core_ids=list(range(N_CORES)))
    r = res.results
    y_prompt = np.stack([r[c]["yp"] for c in range(N_CORES)], axis=0).astype(np.float32)
    y_sample = np.concatenate([r[c]["ys"] for c in range(N_CORES)], axis=0).reshape(128, 1, D).astype(np.float32)
    sc_prompt = np.stack([r[c]["scp"] for c in range(N_CORES)], axis=0).reshape(1, N_CORES, 2, D).astype(np.float32)
    sc_sample = np.concatenate([r[c]["scs"].reshape(NS, 2, D) for c in range(N_CORES)], axis=0).reshape(1, 128, 2, D).astype(np.float32)
    sv_sample = np.concatenate([r[c]["svs"] for c in range(N_CORES)], axis=0).reshape(1, 128, 1, D).astype(np.float32)
    return (y_prompt, y_sample, sc_prompt, sc_sample, sv_sample)
```

```python
from contextlib import ExitStack

import numpy as np
import concourse.bass as bass
import concourse.mybir as mybir
from concourse.bass_utils import run_bass_kernel_spmd

F32 = mybir.dt.float32
F32R = mybir.dt.float32r
BF16 = mybir.dt.bfloat16
AF = mybir.ActivationFunctionType
ALU = mybir.AluOpType

EXACT = False
MMDT = F32 if EXACT else F32R
SAME_ENGINE_SYNC = True

N_CORES = 8
D = 1024
SEQ = 2048
NS = 16
NBLK = 4
NT = 512
EPS = 1e-5
RING = 4

ENGS = ("sync", "scalar", "vector", "gpsimd", "tensor")


class Buf:
    __slots__ = ("name", "writer", "readers")

    def __init__(self, name):
        self.name = name
        self.writer = None
        self.readers = []


class Op:
    __slots__ = ("eng", "fn", "deps", "dma_sem", "tok", "milestone", "idx")

    def __init__(self, eng, fn, dma_sem):
        self.eng = eng
        self.fn = fn
        self.deps = []
        self.dma_sem = dma_sem
        self.tok = None
        self.milestone = False


class Plan:
    def __init__(self):
        self.q = {e: [] for e in ENGS}
        self.dma_count = {}

    def add(self, eng, fn, reads=(), writes=(), dma_sem=None):
        op = Op(eng, fn, dma_sem)
        deps = []
        for b in reads:
            if b.writer is not None:
                deps.append(b.writer)
        for b in writes:
            if b.writer is not None:
                deps.append(b.writer)
            deps.extend(b.readers)
        for b in reads:
            b.readers.append(op)
        for b in writes:
            b.writer = op
            b.readers = []
        seen = set()
        last = {}
        for d in deps:
            if d is op or id(d) in seen:
                continue
            seen.add(id(d))
            if d.dma_sem is not None:
                op.deps.append(d)
                continue
            if d.eng == eng and (eng == "tensor" or not SAME_ENGINE_SYNC):
                continue
            if d.eng not in last or last[d.eng].idx < d.idx:
                last[d.eng] = d
        for d in last.values():
            op.deps.append(d)
            d.milestone = True
        if dma_sem is not None:
            key = id(dma_sem)
            self.dma_count[key] = self.dma_count.get(key, 0) + 16
            op.tok = (dma_sem, self.dma_count[key])
        op.idx = len(self.q[eng])
        self.q[eng].append(op)
        return op

    def unify(self, ops):
        v = max(o.tok[1] for o in ops)
        for o in ops:
            o.tok = (o.tok[0], v)

    def finalize(self, tick_sems):
        for e in ENGS:
            t = 0
            for op in self.q[e]:
                if op.dma_sem is None and op.milestone:
                    t += 1
                    op.tok = (tick_sems[e], t)

    def emit(self, e, eng):
        waited = {}
        for op in self.q[e]:
            need = {}
            for d in op.deps:
                s, v = d.tok
                if waited.get(id(s), 0) >= v:
                    continue
                if need.get(id(s), (None, 0))[1] < v:
                    need[id(s)] = (s, v)
            for s, v in need.values():
                eng.wait_ge(s, v)
                waited[id(s)] = v
            ins = op.fn(eng)
            if op.dma_sem is not None:
                ins.then_inc(op.dma_sem, 16)
            elif op.milestone:
                ins.then_inc(op.tok[0], 1)


class Rot:
    def __init__(self, items):
        self.items = items
        self.i = 0

    def next(self):
        it = self.items[self.i % len(self.items)]
        self.i += 1
        return it


def build_program():
    nc = bass.Bass("TRN2", target_bir_lowering=False)
    nc.dge_precook = False
    P = Plan()

    def din(name, shape, dt=F32):
        return nc.dram_tensor(name, list(shape), dt, kind="ExternalInput").ap()

    def dout(name, shape):
        return nc.dram_tensor(name, list(shape), F32, kind="ExternalOutput").ap()

    xp = din("xp", [SEQ, D])
    xs = din("xs", [NS, D])
    sc = din("sc", [NS * 2, D])
    gn_d = din("gn", [128, 8])
    cw_d = din("cw", [128, 8 * 3])
    wt_d = din("wt", [18, 128, 8 * 512], MMDT)
    wsT_d = din("wsT", [128, 8 * 128])
    bs_d = din("bs", [8 * 128])
    w00_d = din("w00", [8])
    b0_d = din("b0", [8])
    gv_d = din("gv", [D])
    gf_d = din("gf", [D])
    yp = dout("yp", [SEQ, D])
    ys = dout("ys", [NS, D])
    scp = dout("scp", [2, D])
    scs = dout("scs", [NS, 2 * D])
    svs = dout("svs", [NS, D])

    es = ExitStack()
    with es:
        def sb(name, shape, dt=F32):
            return es.enter_context(nc.sbuf_tensor(name, list(shape), dt))

        ring = [sb(f"ring{i}", [128, 8, 512], MMDT) for i in range(RING)]
        hT = sb("hT", [128, 8, NT + NS], MMDT)
        hTs = hT[:, :, NT:NT + NS]
        vn = sb("vn", [128, 4, D], MMDT)
        vs32 = sb("vs32", [NS, D], F32)
        mixT = sb("mixT", [128, 16, NT], MMDT)
        mixTs = sb("mixTs", [128, 16, NS], MMDT)
        tok = [sb(f"tok{i}", [128, D], F32) for i in range(5)]
        xin = [sb(f"xin{i}", [128, D], F32) for i in range(3)]
        junk = sb("junk", [128, D], BF16)
        fa_all = sb("fa_all", [128, 2, NT], F32)
        fa = [fa_all[:, i, :] for i in range(2)]
        fb = [sb(f"fb{i}", [128, NT + 2], F32) for i in range(2)]
        fc = [sb(f"fc{i}", [128, NT], F32) for i in range(2)]
        fd = [sb(f"fd{i}", [128, NT], F32) for i in range(2)]
        sa = [sb(f"sa{i}", [128, NS], F32) for i in range(2)]
        sbb = [sb(f"sbb{i}", [128, NS], F32) for i in range(2)]
        scc = [sb(f"scc{i}", [128, NS], F32) for i in range(2)]
        sd = [sb(f"sd{i}", [128, NS], F32) for i in range(2)]
        hbs = sb("hbs", [128, 8, NS], F32)
        stT = sb("stT", [128, 8, 2 * NS], F32)
        carry = sb("carry", [128, 8, 2], F32)
        gvbc = sb("gvbc", [128, D], F32)
        gfbc = sb("gfbc", [128, D], F32)
        WT = sb("WT", [128, 8, 128], MMDT)
        bbc = sb("bbc", [128, 8, 128], F32)
        onesr = sb("onesr", [1, 128], MMDT)
        ones32 = sb("ones32", [128, 128], F32)
        ident = sb("ident", [128, 128], F32)
        gn = sb("gnt", [128, 8], F32)
        cw = sb("cwt", [128, 8 * 3], F32)
        w00 = sb("w00t", [128, 8], F32)
        b0 = sb("b0t", [128, 8], F32)
        mhalf = sb("mhalf", [128, 1], F32)
        stat = sb("stat", [128, 18], F32)
        ps = es.enter_context(nc.psum_tensor("ps", [128, 8, 512], F32))

        sem = lambda n: es.enter_context(nc.semaphore(n))
        tick = {e: sem(f"tick_{e}") for e in ENGS if e != "sync"}
        s_ring = [sem(f"s_ring{i}") for i in range(RING)]
        s_const = sem("s_const")
        s_const2 = sem("s_const2")
        s_xin = [sem(f"s_xin{i}") for i in range(3)]
        s_tokld = [sem(f"s_tokld{i}") for i in range(5)]
        s_tokst = [sem(f"s_tokst{i}") for i in range(5)]
        s_out = sem("s_out")
        block = es.enter_context(nc.Block())

        B = Buf
        b_ring = [[B(f"ring{i}q{q}") for q in range(4)] for i in range(RING)]
        b_hT = [B(f"hT{t}") for t in range(4)]
        b_hTs = B("hTs")
        b_vn = [B(f"vn{t}") for t in range(4)]
        b_vs32 = B("vs32")
        b_mixT = [B(f"mixT{k}") for k in range(16)]
        b_mixTs = [B(f"mixTs{k}") for k in range(16)]
        b_tok = [B(f"tok{i}") for i in range(5)]
        b_xin = [B(f"xin{i}") for i in range(3)]
        b_junk = B("junk")
        b_fa = [B(f"fa{i}") for i in range(2)]
        b_fb = [B(f"fb{i}") for i in range(2)]
        b_fc = [B(f"fc{i}") for i in range(2)]
        b_fd = [B(f"fd{i}") for i in range(2)]
        b_sa = [B(f"sa{i}") for i in range(2)]
        b_sb = [B(f"sbb{i}") for i in range(2)]
        b_sc = [B(f"scc{i}") for i in range(2)]
        b_sd = [B(f"sd{i}") for i in range(2)]
        b_hbs = [B(f"hbs{j}") for j in range(8)]
        b_stT = B("stT")
        b_carry = [B(f"carry{j}") for j in range(8)]
        b_const = B("const")
        b_ps = [B(f"ps{i}") for i in range(8)]
        b_stat = [B(f"stat{i}") for i in range(8)]

        rF = Rot([0, 1, 2, 7])
        rT = Rot([4, 5, 6])
        BANK_S = 7
        BANK_M = 3
        r_fa, r_fb, r_fc, r_fd = Rot([0, 1]), Rot([0, 1]), Rot([0, 1]), Rot([0, 1])
        r_sa, r_sb, r_sc, r_sd = Rot([0, 1]), Rot([0, 1]), Rot([0, 1]), Rot([0, 1])
        r_xin = Rot([0, 1, 2])
        r_stat = Rot(list(range(8)))
        r_tokgv = Rot([0, 1])

        b_cd = {n: B("cd_" + n) for n in ("gn", "cw", "w00", "b0", "gv", "gf", "wsT", "bs")}
        c_ops = []

        c_ops2 = []

        def cdma(name, out, in_, late=False):
            (c_ops2 if late else c_ops).append(
                P.add("sync", lambda e, o=out, i_=in_: e.dma_start(out=o, in_=i_),
                      writes=[b_cd[name]], dma_sem=(s_const2 if late else s_const)))

        def const_dmas_early():
            cdma("gn", gn[:], gn_d)
            cdma("gv", gvbc[:], gv_d.partition_broadcast(128))

        def const_dmas_late():
            cdma("cw", cw[:], cw_d, late=True)
            cdma("w00", w00[:], w00_d.partition_broadcast(128), late=True)
            cdma("b0", b0[:], b0_d.partition_broadcast(128), late=True)
            cdma("gf", gfbc[:], gf_d.partition_broadcast(128), late=True)
            cdma("wsT", tok[4][:], wsT_d, late=True)
            cdma("bs", bbc[:].rearrange("p h t -> p (h t)"), bs_d.partition_broadcast(128), late=True)

        b_setup = B("setup")
        b_ident, b_WT = B("ident"), B("WT")

        def const_compute():
            P.add("gpsimd", lambda e: e.affine_select(out=WT[:].rearrange("p h t -> p (h t)"), in_=tok[4][:],
                                                      pattern=[[0, 8], [1, 128]], compare_op=ALU.is_ge,
                                                      fill=0.0, base=0, channel_multiplier=-1),
                  reads=[b_cd["wsT"]], writes=[b_WT, b_tok[4]])
            P.add("vector", lambda e: e.tensor_copy(out=onesr[:], in_=ones32[0:1, :]),
                  reads=[b_setup, b_ident, b_WT] + list(b_cd.values()), writes=[b_const])

        P.add("gpsimd", lambda e: e.memset(mhalf[:], -0.5), writes=[b_setup])
        P.add("gpsimd", lambda e: e.memset(ones32[:], 1.0), writes=[b_setup])
        P.add("gpsimd", lambda e: e.memset(carry[:], 0.0), writes=b_carry)
        P.add("scalar", lambda e: e.activation(out=stat[:, 16:17], in_=mhalf[:, 0:1], func=AF.Gelu),
              reads=[b_setup], writes=[B("warm")])
        P.add("gpsimd", lambda e: e.affine_select(out=ident[:], in_=ones32[:], pattern=[[1, 128]],
                                                  compare_op=ALU.is_equal, fill=0.0, base=0,
                                                  channel_multiplier=-1),
              reads=[b_setup], writes=[b_ident])

        wtiles = []
        for _b in range(NBLK):
            wtiles += [("v", 0), ("v", 1)]
            wtiles += [("B", j) for j in range(8)]
            wtiles += [("A", i) for i in range(4)]
            wtiles += [("o", 0, 0), ("o", 0, 1), ("o", 1, 0), ("o", 1, 1)]
        n_w = len(wtiles)
        w_issued = [0]

        def issue_weight(idx):
            slot = idx % RING
            rt = ring[slot]
            P.add("sync", lambda e, rt=rt, idx=idx: e.dma_start(out=rt[:].rearrange("p k c -> p (k c)"),
                                                               in_=wt_d[idx % 18]),
                  writes=b_ring[slot], dma_sem=s_ring[slot])

        def prefetch(upto):
            while w_issued[0] < min(upto, n_w):
                issue_weight(w_issued[0])
                w_issued[0] += 1

        w_cursor = [0]

        def next_weight(expect):
            idx = w_cursor[0]
            assert wtiles[idx][0] == expect, (wtiles[idx], expect)
            assert idx < w_issued[0]
            w_cursor[0] += 1
            return idx % RING

        def release_weights():
            prefetch(w_cursor[0] + RING)

        class Group:
            pass

        def make_prompt_group(b):
            g = Group()
            g.kind = "P"
            g.n = NT
            g.tiles = [(t * 128, 128) for t in range(4)]
            g.hT = hT
            g.b_hT = b_hT
            g.mixT = mixT
            g.b_mixT = b_mixT
            g.x_rows = lambda t: xp[b * NT + t * 128:b * NT + (t + 1) * 128, :]
            g.y_rows = lambda t: yp[b * NT + t * 128:b * NT + (t + 1) * 128, :]
            g.tokidx = [0, 1, 2, 3]
            g.blk = b
            return g

        def make_sample_group():
            g = Group()
            g.kind = "S"
            g.n = NS
            g.tiles = [(0, NS)]
            g.hT = hTs
            g.b_hT = [b_hTs]
            g.mixT = mixTs
            g.b_mixT = b_mixTs
            g.x_rows = lambda t: xs[:, :]
            g.y_rows = lambda t: ys[:, :]
            g.tokidx = [4]
            g.blk = 0
            return g

        def rms_inv(src_ap, nr, b_src, ncols):
            si = r_stat.next()
            ssq = stat[:nr, 2 * si:2 * si + 1]
            rinv = stat[:nr, 2 * si + 1:2 * si + 2]
            bst = b_stat[si]
            P.add("scalar", lambda e: e.activation(out=junk[:nr, 0:ncols], in_=src_ap, func=AF.Square,
                                                   accum_out=ssq),
                  reads=(b_src if isinstance(b_src, list) else [b_src]), writes=[b_junk, bst])
            P.add("gpsimd", lambda e: e.tensor_scalar(out=ssq, in0=ssq, scalar1=1.0 / ncols, scalar2=EPS,
                                                      op0=ALU.mult, op1=ALU.add),
                  reads=[bst], writes=[bst])
            P.add("gpsimd", lambda e: e.tensor_tensor(out=rinv, in0=ssq, in1=mhalf[:nr, :], op=ALU.pow),
                  reads=[bst, b_setup], writes=[bst])
            return rinv, bst

        class XSrc:
            def __init__(self, t, b, sm):
                self.t, self.s = t, sm
                self.b = list(b) if isinstance(b, (list, tuple)) else [b]

        xsrc_xin = [XSrc(xin[i], b_xin[i], s_xin[i]) for i in range(3)]
        xsrc_tok = [XSrc(tok[i], b_tok[i], s_tokld[i]) for i in range(5)]
        s_vnld = sem("s_vnld")
        xsrc_vn3 = XSrc(fa_all[:].rearrange("p a n -> p (a n)"), [b_fa[0], b_fa[1]], s_vnld)

        def xprep_load(g, t, src=None, queue="sync"):
            xs_ = xsrc_xin[r_xin.next()] if src is None else src
            P.add(queue, lambda e: e.dma_start(out=xs_.t[:g.tiles[t][1], :], in_=g.x_rows(t)),
                  writes=xs_.b, dma_sem=xs_.s)
            return xs_

        pre_state = {}

        def xprep_pre_a(g, t, xi):
            c0, nr = g.tiles[t]
            xa = xi.t
            pre_state[(id(g), t)] = rms_inv(xa[:nr, :], nr, xi.b, D)

        def xprep_pre_b(g, t, xi):
            c0, nr = g.tiles[t]
            xa = xi.t
            rinv, bst = pre_state.pop((id(g), t))
            P.add("scalar", lambda e: e.activation(out=xa[:nr, :], in_=xa[:nr, :], func=AF.Copy, scale=rinv),
                  reads=[bst] + xi.b, writes=xi.b)

        def xprep_pre(g, t, xi):
            xprep_pre_a(g, t, xi)
            xprep_pre_b(g, t, xi)

        def xprep_post(g, t, xi):
            c0, nr = g.tiles[t]
            xa = xi.t
            for hf in range(2):
                bank = rF.next()
                for q in range(4):
                    k = hf * 4 + q
                    P.add("tensor", lambda e, k=k, q=q, bank=bank: e.transpose(
                        out=ps[:, bank, q * 128:q * 128 + nr], in_=xa[:nr, k * 128:(k + 1) * 128],
                        identity=ident[:nr, :nr]),
                        reads=xi.b + [b_ident], writes=[b_ps[bank]])
                P.add("vector", lambda e, hf=hf, bank=bank: e.tensor_tensor(
                    out=g.hT[:, hf * 4:hf * 4 + 4, c0:c0 + nr],
                    in0=ps[:, bank, :].rearrange("p (q t) -> p q t", q=4)[:, :, 0:nr],
                    in1=gn[:, hf * 4:hf * 4 + 4].unsqueeze(2).to_broadcast([128, 4, nr]), op=ALU.mult),
                    reads=[b_ps[bank], b_cd["gn"]], writes=[g.b_hT[t]])

        def xprep_compute(g, t, xi):
            xprep_pre(g, t, xi)
            xprep_post(g, t, xi)

        def load_state_dma():
            P.add("sync", lambda e: e.dma_start(out=tok[2][:2 * NS, :], in_=sc),
                  writes=[b_tok[2]], dma_sem=s_tokld[2])
            sc3 = sc.rearrange("(s r) d -> s r d", r=2)
            P.add("sync", lambda e: e.dma_start(out=scs[:, 0:D], in_=sc3[:, 1, :]), dma_sem=s_out)

        def load_state():
            for hf in range(2):
                bank = rF.next()
                for q in range(4):
                    k = hf * 4 + q
                    P.add("tensor", lambda e, k=k, q=q, bank=bank: e.transpose(
                        out=ps[:, bank, q * 128:q * 128 + 2 * NS], in_=tok[2][:2 * NS, k * 128:(k + 1) * 128],
                        identity=ident[:2 * NS, :2 * NS]),
                        reads=[b_tok[2], b_ident], writes=[b_ps[bank]])
                for q in range(4):
                    k = hf * 4 + q
                    P.add("vector", lambda e, k=k, q=q, bank=bank: e.tensor_copy(
                        out=stT[:, k, :], in_=ps[:, bank, q * 128:q * 128 + 2 * NS]),
                        reads=[b_ps[bank]], writes=[b_stT])

        def v_tile(g, t, slots):
            c0, nr = g.tiles[t]
            gi = r_tokgv.next()
            gvt = tok[gi]
            for half in range(2):
                bank = rT.next()
                rt = ring[slots[half]]
                for k in range(8):
                    P.add("tensor", lambda e, k=k, bank=bank, rt=rt: e.matmul(
                        ps[:nr, bank, :], lhsT=g.hT[:, k, c0:c0 + nr], rhs=rt[:, k, :],
                        start=(k == 0), stop=(k == 7)),
                        reads=[g.b_hT[t]] + b_ring[slots[half]], writes=[b_ps[bank]])
                P.add("scalar", lambda e, half=half, bank=bank: e.activation(
                    out=gvt[:nr, half * 512:(half + 1) * 512], in_=ps[:nr, bank, :], func=AF.Gelu),
                    reads=[b_ps[bank]], writes=[b_tok[gi]])
            rinv, bst = rms_inv(gvt[:nr, :], nr, b_tok[gi], D)
            if g.kind == "P":
                P.add("vector", lambda e: e.scalar_tensor_tensor(
                    out=vn[:nr, t, :], in0=gvt[:nr, :], scalar=rinv, in1=gvbc[:nr, :],
                    op0=ALU.mult, op1=ALU.mult),
                    reads=[b_tok[gi], bst, b_cd["gv"]], writes=[b_vn[t]])
            else:
                P.add("vector", lambda e: e.scalar_tensor_tensor(
                    out=vs32[:nr, :], in0=gvt[:nr, :], scalar=rinv, in1=gvbc[:nr, :],
                    op0=ALU.mult, op1=ALU.mult),
                    reads=[b_tok[gi], bst, b_cd["gv"]], writes=[b_vs32])
                P.add("sync", lambda e: e.dma_start(out=svs, in_=vs32[:, :]),
                      reads=[b_vs32], dma_sem=s_out)

        def v_phase(groups, slots):
            for g in groups:
                for t in range(len(g.tiles)):
                    v_tile(g, t, slots)

        def feat_mm(g, slot, q, col0):
            bank = rF.next()
            rt = ring[slot]
            for k in range(8):
                P.add("tensor", lambda e, k=k: e.matmul(
                    ps[:, bank, 0:g.n], lhsT=rt[:, k, col0:col0 + 128], rhs=g.hT[:, k, 0:g.n],
                    start=(k == 0), stop=(k == 7)),
                    reads=g.b_hT + [b_ring[slot][q]], writes=[b_ps[bank]])
            return bank

        with_sample = [False]
        samp_piece = {}

        rF2 = Rot([0, 1, 2, 7, 4, 5])

        def feat_mm2(slot, q, col0):
            bx, by = rF2.next(), rF2.next()
            rt = ring[slot]
            for k in range(8):
                P.add("tensor", lambda e, k=k: e.matmul(
                    ps[:, bx, 0:256], lhsT=rt[:, k, col0:col0 + 128], rhs=hT[:, k, 0:256],
                    start=(k == 0), stop=(k == 7)),
                    reads=[b_hT[0], b_hT[1], b_ring[slot][q]], writes=[b_ps[bx]])
                P.add("tensor", lambda e, k=k: e.matmul(
                    ps[:, by, 0:256 + NS], lhsT=rt[:, k, col0:col0 + 128], rhs=hT[:, k, 256:NT + NS],
                    start=(k == 0), stop=(k == 7)),
                    reads=[b_hT[2], b_hT[3], b_hTs, b_ring[slot][q]], writes=[b_ps[by]])
            return bx, by

        def seg_src(g, slot, q, col0):
            if g.kind == "P":
                if with_sample[0]:
                    bx, by = feat_mm2(slot, q, col0)
                    samp_piece[q] = (ps[:, by, 256:256 + NS], b_ps[by])
                    return [(ps[:, bx, 0:256], b_ps[bx], 0, 256), (ps[:, by, 0:256], b_ps[by], 256, 512)]
                bank = feat_mm(g, slot, q, col0)
                return [(ps[:, bank, 0:g.n], b_ps[bank], 0, g.n)]
            sap, sbuf = samp_piece.pop(q)
            return [(sap, sbuf, 0, NS)]

        def interleave(gens):
            gens = list(gens)
            while gens:
                for gen in list(gens):
                    try:
                        next(gen)
                    except StopIteration:
                        gens.remove(gen)

        def b_side(groups, slot, j):
            interleave([b_side_g(g, slot, j) for g in groups])

        def b_side_g(g, slot, j):
            n = g.n
            isP = g.kind == "P"
            ia = (r_fa if isP else r_sa).next()
            ib = (r_fb if isP else r_sb).next()
            ic = (r_fc if isP else r_sc).next()
            idd = (r_fd if isP else r_sd).next()
            ta, b_ta = (fa[ia], b_fa[ia]) if isP else (sa[ia], b_sa[ia])
            tc, b_tc = (fc[ic], b_fc[ic]) if isP else (scc[ic], b_sc[ic])
            td, b_td = (fd[idd], b_fd[idd]) if isP else (sd[idd], b_sd[idd])
            w0, w1, w2 = (cw[:, 3 * j + i:3 * j + i + 1] for i in range(3))
            for sap, sbuf, c0, c1 in seg_src(g, slot, 0, 0):
                P.add("scalar", lambda e, sap=sap, c0=c0, c1=c1: e.activation(out=ta[:, c0:c1], in_=sap, func=AF.Copy),
                      reads=[sbuf], writes=[b_ta])
            yield
            pieces = seg_src(g, slot, 2, 256)
            if isP:
                tb, b_tb = fb[ib], b_fb[ib]
                P.add("vector", lambda e: e.tensor_copy(out=tb[:, 0:2], in_=carry[:, j, :]),
                      reads=[b_carry[j]], writes=[b_tb])
                for sap, sbuf, c0, c1 in pieces:
                    P.add("vector", lambda e, sap=sap, c0=c0, c1=c1: e.tensor_tensor(
                        out=tb[:, 2 + c0:2 + c1], in0=sap, in1=ta[:, c0:c1], op=ALU.mult),
                        reads=[sbuf, b_ta], writes=[b_tb])
                h0, h1, h2 = tb[:, 0:n], tb[:, 1:n + 1], tb[:, 2:n + 2]
                rd = [b_tb]
            else:
                for sap, sbuf, c0, c1 in pieces:
                    P.add("vector", lambda e, sap=sap: e.tensor_tensor(out=hbs[:, j, :], in0=sap,
                                                                     in1=ta[:, 0:n], op=ALU.mult),
                          reads=[sbuf, b_ta], writes=[b_hbs[j]])
                st3 = stT[:, j, :].rearrange("p (s r) -> p s r", r=2)
                h0, h1, h2 = st3[:, :, 0], st3[:, :, 1], hbs[:, j, :]
                rd = [b_hbs[j], b_stT]
            yield
            if isP:
                P.add("vector", lambda e: e.tensor_copy(out=carry[:, j, :], in_=tb[:, n:n + 2]),
                      reads=[b_tb], writes=[b_carry[j]])
            P.add("vector", lambda e: e.tensor_scalar(out=tc[:, 0:n], in0=h0, scalar1=w0, scalar2=None,
                                                      op0=ALU.mult),
                  reads=rd + [b_const], writes=[b_tc])
            P.add("vector", lambda e: e.scalar_tensor_tensor(out=tc[:, 0:n], in0=h1, scalar=w1,
                                                             in1=tc[:, 0:n], op0=ALU.mult, op1=ALU.add),
                  reads=rd + [b_const, b_tc], writes=[b_tc])
            P.add("vector", lambda e: e.scalar_tensor_tensor(out=tc[:, 0:n], in0=h2, scalar=w2,
                                                             in1=tc[:, 0:n], op0=ALU.mult, op1=ALU.add),
                  reads=rd + [b_const, b_tc], writes=[b_tc])
            for sap, sbuf, c0, c1 in seg_src(g, slot, 3, 384):
                P.add("scalar", lambda e, sap=sap, c0=c0, c1=c1: e.activation(out=td[:, c0:c1], in_=sap, func=AF.Tanh,
                                                                              scale=0.5),
                      reads=[sbuf], writes=[b_td])
                P.add("vector", lambda e, sap=sap, c0=c0, c1=c1: e.scalar_tensor_tensor(
                    out=td[:, c0:c1], in0=td[:, c0:c1], scalar=1.0, in1=sap, op0=ALU.add, op1=ALU.mult),
                    reads=[sbuf, b_td], writes=[b_td])
            yield
            for sap, sbuf, c0, c1 in seg_src(g, slot, 1, 128):
                P.add("vector", lambda e, sap=sap, c0=c0, c1=c1: e.tensor_tensor(
                    out=tc[:, c0:c1], in0=sap, in1=tc[:, c0:c1], op=ALU.mult),
                    reads=[sbuf, b_tc], writes=[b_tc])
            P.add("vector", lambda e: e.scalar_tensor_tensor(out=g.mixT[:, 8 + j, 0:n], in0=tc[:, 0:n],
                                                             scalar=0.5, in1=td[:, 0:n], op0=ALU.mult,
                                                             op1=ALU.mult),
                  reads=[b_tc, b_td], writes=[g.b_mixT[8 + j]])

        def a_side(groups, slot, h):
            interleave([a_side_g(g, slot, h) for g in groups])

        def a_side_g(g, slot, h):
            hl = h % 2
            n = g.n
            isP = g.kind == "P"
            ia = (r_fa if isP else r_sa).next()
            idd = (r_fd if isP else r_sd).next()
            ta, b_ta = (fa[ia], b_fa[ia]) if isP else (sa[ia], b_sa[ia])
            td, b_td = (fd[idd], b_fd[idd]) if isP else (sd[idd], b_sd[idd])
            for sap, sbuf, c0, c1 in seg_src(g, slot, hl, hl * 128):
                P.add("scalar", lambda e, sap=sap, c0=c0, c1=c1: e.activation(out=ta[:, c0:c1], in_=sap, func=AF.Gelu),
                      reads=[sbuf], writes=[b_ta])
            yield
            for sap, sbuf, c0, c1 in seg_src(g, slot, 2 + hl, 256 + hl * 128):
                P.add("scalar", lambda e, sap=sap, c0=c0, c1=c1: e.activation(out=td[:, c0:c1], in_=sap, func=AF.Tanh,
                                                                              scale=0.5),
                      reads=[sbuf], writes=[b_td])
                P.add("vector", lambda e, sap=sap, c0=c0, c1=c1: e.scalar_tensor_tensor(
                    out=td[:, c0:c1], in0=td[:, c0:c1], scalar=1.0, in1=sap, op0=ALU.add, op1=ALU.mult),
                    reads=[sbuf, b_td], writes=[b_td])
            yield
            if isP:
                for c in range(4):
                    P.add("tensor", lambda e, c=c: e.matmul(
                        ps[:, BANK_M, c * 128:(c + 1) * 128], lhsT=vn[:, c, h * 128:(h + 1) * 128],
                        rhs=WT[:, h, :], start=True, stop=True),
                        reads=[b_vn[c], b_const], writes=[b_ps[BANK_M]])
                ic = r_fc.next()
                tc, b_tc = fc[ic], b_fc[ic]
                P.add("vector", lambda e: e.tensor_tensor(
                    out=tc[:, 0:n].rearrange("p (c t) -> p c t", c=4),
                    in0=ps[:, BANK_M, 0:n].rearrange("p (c t) -> p c t", c=4),
                    in1=bbc[:, h, :].unsqueeze(1).to_broadcast([128, 4, 128]), op=ALU.add),
                    reads=[b_ps[BANK_M], b_const], writes=[b_tc])
                P.add("vector", lambda e: e.tensor_tensor(out=ta[:, 0:n], in0=tc[:, 0:n],
                                                          in1=ta[:, 0:n], op=ALU.mult),
                      reads=[b_tc, b_ta], writes=[b_ta])
            else:
                cbank = 6
                cmap = ps[:, cbank, 0:NS]
                P.add("tensor", lambda e: e.transpose(
                    out=cmap, in_=vs32[:n, h * 128:(h + 1) * 128], identity=ident[:n, :n]),
                    reads=[b_vs32, b_const], writes=[b_ps[cbank]])
                ic = r_sc.next()
                tc, b_tc = scc[ic], b_sc[ic]
                P.add("vector", lambda e: e.tensor_scalar(out=tc[:, 0:n], in0=cmap,
                                                          scalar1=w00[:, h:h + 1], scalar2=b0[:, h:h + 1],
                                                          op0=ALU.mult, op1=ALU.add),
                      reads=[b_ps[cbank], b_const], writes=[b_tc])
                P.add("vector", lambda e: e.tensor_tensor(out=ta[:, 0:n], in0=tc[:, 0:n], in1=ta[:, 0:n],
                                                          op=ALU.mult),
                      reads=[b_tc, b_ta], writes=[b_ta])
            P.add("vector", lambda e: e.scalar_tensor_tensor(out=g.mixT[:, h, 0:n], in0=ta[:, 0:n], scalar=0.5,
                                                             in1=td[:, 0:n], op0=ALU.mult, op1=ALU.mult),
                  reads=[b_ta, b_td], writes=[g.b_mixT[h]])

        def load_resid(groups):
            for g in groups:
                for t, (c0, nr) in enumerate(g.tiles):
                    ti = g.tokidx[t]
                    P.add("sync", lambda e, t=t, ti=ti, nr=nr, g=g: e.dma_start(out=tok[ti][:nr, :], in_=g.x_rows(t)),
                          writes=[b_tok[ti]], dma_sem=s_tokld[ti])

        def out_tile(g, t, half, slots):
            c0, nr = g.tiles[t]
            ti = g.tokidx[t]
            bank = rT.next()
            korder = list(range(8, 16)) + list(range(8))
            for i_, k in enumerate(korder):
                rt = ring[slots[k // 8]]
                P.add("tensor", lambda e, k=k, rt=rt, i_=i_: e.matmul(
                    ps[:nr, bank, :], lhsT=g.mixT[:, k, c0:c0 + nr], rhs=rt[:, k % 8, :],
                    start=(i_ == 0), stop=(i_ == 15)),
                    reads=[g.b_mixT[k]] + b_ring[slots[k // 8]], writes=[b_ps[bank]])
            yt = tok[ti]
            P.add("vector", lambda e: e.tensor_tensor(
                out=yt[:nr, half * 512:(half + 1) * 512], in0=ps[:nr, bank, :],
                in1=yt[:nr, half * 512:(half + 1) * 512], op=ALU.add),
                reads=[b_ps[bank], b_tok[ti]], writes=[b_tok[ti]])
            if half == 1:
                rinv, bst = rms_inv(yt[:nr, :], nr, b_tok[ti], D)

                def finish():
                    P.add("vector", lambda e: e.scalar_tensor_tensor(
                        out=yt[:nr, :], in0=yt[:nr, :], scalar=rinv, in1=gfbc[:nr, :],
                        op0=ALU.mult, op1=ALU.mult),
                        reads=[b_tok[ti], bst, b_const], writes=[b_tok[ti]])
                    P.add("sync", lambda e: e.dma_start(out=g.y_rows(t), in_=yt[:nr, :]),
                          reads=[b_tok[ti]], dma_sem=s_tokst[ti])
                return finish
            return None

        def out_proj_half(groups, half, slots, between=None):
            pending = None
            for g in groups:
                for t in range(len(g.tiles)):
                    fin = out_tile(g, t, half, slots)
                    if pending is not None:
                        pending()
                    pending = fin
                    if between is not None:
                        between(g, t)
            if pending is not None:
                pending()

        def emit_state_outputs_sample():
            banks = [rT.next(), rT.next()]
            for k in range(8):
                bank = banks[k // 4]
                q = k % 4
                P.add("tensor", lambda e, k=k, q=q, bank=bank: e.transpose(
                    out=ps[:NS, bank, q * 128:(q + 1) * 128], in_=hbs[:, k, :], identity=ident[:, :]),
                    reads=[b_hbs[k], b_const], writes=[b_ps[bank]])
            for hf in range(2):
                P.add("scalar", lambda e, hf=hf: e.activation(out=tok[4][:NS, hf * 512:(hf + 1) * 512],
                                                              in_=ps[:NS, banks[hf], :], func=AF.Copy),
                      reads=[b_ps[banks[hf]]], writes=[b_tok[4]])
            P.add("sync", lambda e: e.dma_start(out=scs[:, D:2 * D], in_=tok[4][:NS, :]),
                  reads=[b_tok[4]], dma_sem=s_tokst[4])

        def emit_state_outputs_prompt():
            banks = [rT.next(), rT.next()]
            for k in range(8):
                bank = banks[k // 4]
                q = k % 4
                P.add("tensor", lambda e, k=k, q=q, bank=bank: e.transpose(
                    out=ps[:2, bank, q * 128:(q + 1) * 128], in_=carry[:, k, :], identity=ident[:, :]),
                    reads=[b_carry[k], b_const], writes=[b_ps[bank]])
            for hf in range(2):
                P.add("scalar", lambda e, hf=hf: e.activation(out=tok[4][:2, hf * 512:(hf + 1) * 512],
                                                              in_=ps[:2, banks[hf], :], func=AF.Copy),
                      reads=[b_ps[banks[hf]]], writes=[b_tok[4]])
            P.add("sync", lambda e: e.dma_start(out=scp, in_=tok[4][:2, :]),
                  reads=[b_tok[4]], dma_sem=s_tokst[4])

        gS = make_sample_group()
        g0 = make_prompt_group(0)
        const_dmas_early()
        xi_s = xprep_load(gS, 0, xsrc_tok[3])
        pend = [xprep_load(g0, 0), xprep_load(g0, 1), xprep_load(g0, 2, xsrc_tok[0]), xprep_load(g0, 3, xsrc_tok[1])]
        load_state_dma()
        prefetch(2)
        const_dmas_late()
        P.unify(c_ops)
        P.unify(c_ops2)
        prefetch(RING)
        xprep_pre_a(gS, 0, xi_s)
        xprep_pre_a(g0, 0, pend[0])
        xprep_pre_b(gS, 0, xi_s)
        xprep_pre_a(g0, 1, pend[1])
        xprep_pre_b(g0, 0, pend[0])
        xprep_post(gS, 0, xi_s)
        for t in range(4):
            if t + 2 < 4:
                xprep_pre_a(g0, t + 2, pend[t + 2])
            if t + 1 < 4:
                xprep_pre_b(g0, t + 1, pend[t + 1])
            xprep_post(g0, t, pend[t])
        const_compute()

        for b in range(NBLK):
            gP = make_prompt_group(b)
            groups = [gP, gS] if b == 0 else [gP]
            with_sample[0] = (b == 0)
            s0 = next_weight("v")
            s1 = next_weight("v")
            v_phase(groups, [s0, s1])
            release_weights()
            if b == 0:
                load_state()
            for j in range(8):
                s = next_weight("B")
                b_side(groups, s, j)
                release_weights()
                if j == 3 and b + 1 < NBLK:
                    gN = make_prompt_group(b + 1)
                    pend = [xprep_load(gN, 0), xprep_load(gN, 1), xprep_load(gN, 2)]
                if j == 5:
                    load_resid([gP])
            if b == 0:
                emit_state_outputs_sample()
                load_resid([gS])
            if b == NBLK - 1:
                emit_state_outputs_prompt()
            for i in range(4):
                s = next_weight("A")
                a_side(groups, s, 2 * i)
                a_side(groups, s, 2 * i + 1)
                release_weights()
                if b + 1 < NBLK:
                    if i == 0:
                        for t_ in range(len(pend)):
                            xprep_pre_a(gN, t_, pend[t_])
                    else:
                        xprep_pre_b(gN, i - 1, pend[i - 1])
            if b + 1 < NBLK:
                pend.append(xprep_load(gN, 3, xsrc_vn3))
                xprep_pre_a(gN, 3, pend[3])
            so = [next_weight("o"), next_weight("o")]
            out_proj_half(groups, 0, so)
            release_weights()
            if b + 1 < NBLK:
                xprep_pre_b(gN, 3, pend[3])
            so = [next_weight("o"), next_weight("o")]
            if b + 1 < NBLK:
                def between(g, t, gN=gN, pend=pend):
                    if g.kind != "P":
                        return
                    xprep_post(gN, t, pend[t])
                out_proj_half(groups, 1, so, between)
            else:
                out_proj_half(groups, 1, so)
            release_weights()

        P.finalize(tick)
        final_waits = [(s_out, P.dma_count.get(id(s_out), 0))]
        for s in s_tokst + s_xin:
            final_waits.append((s, P.dma_count.get(id(s), 0)))

        @block.sync
        def _(eng):
            P.emit("sync", eng)
            for s, v in final_waits:
                if v > 0:
                    eng.wait_ge(s, v)

        @block.scalar
        def _(eng):
            P.emit("scalar", eng)

        @block.vector
        def _(eng):
            P.emit("vector", eng)

        @block.gpsimd
        def _(eng):
            P.emit("gpsimd", eng)

        @block.tensor
        def _(eng):
            P.emit("tensor", eng)

    return nc


_NC_CACHE = {}


def kernel(x_prompt, x_sample, state_conv, g_norm, w_in, w_s, b_s, g_v, conv_w, w_out, g_final):
    f = lambda a: np.ascontiguousarray(np.asarray(a, dtype=np.float32))
    x_prompt, x_sample, state_conv = f(x_prompt), f(x_sample), f(state_conv)
    if "nc" not in _NC_CACHE:
        _NC_CACHE["nc"] = build_program()
    nc = _NC_CACHE["nc"]
    gn = f(np.asarray(g_norm)[0].reshape(8, 128).T)
    cw = f(np.asarray(conv_w)[0].reshape(3, 8, 128).transpose(2, 1, 0).reshape(128, 24))
    wsT = f(np.asarray(w_s)[0].transpose(2, 0, 1).reshape(128, 8 * 128))
    bs = f(np.asarray(b_s)[0].reshape(8 * 128))
    w00 = f(np.asarray(w_s)[0, :, 0, 0])
    b0 = f(np.asarray(b_s)[0, :, 0])
    wi = np.asarray(w_in, dtype=np.float32)[0]
    wo = np.asarray(w_out, dtype=np.float32)[0]
    tiles = []
    for half in range(2):
        tiles.append(wi[:, D + half * 512:D + (half + 1) * 512])
    for j in range(8):
        tiles.append(np.concatenate([wi[:, 3 * D + sg * D + j * 128:3 * D + sg * D + (j + 1) * 128] for sg in range(4)], axis=1))
    for i in range(4):
        tiles.append(np.concatenate([wi[:, i * 256:(i + 1) * 256], wi[:, 2 * D + i * 256:2 * D + (i + 1) * 256]], axis=1))
    for half in range(2):
        for kh in range(2):
            tiles.append(wo[kh * D:(kh + 1) * D, half * 512:(half + 1) * 512])
    wt = np.ascontiguousarray(np.stack([t_.reshape(8, 128, 512).transpose(1, 0, 2).reshape(128, 8 * 512) for t_ in tiles], axis=0))
    shared = {
        "gn": gn, "cw": cw, "wt": wt,
        "wsT": wsT, "bs": bs, "w00": w00, "b0": b0, "gv": f(np.asarray(g_v)[0]), "gf": f(g_final),
    }
    in_maps = []
    for c in range(N_CORES):
        m = dict(shared)
        m["xp"] = x_prompt[c]
        m["xs"] = f(x_sample[c * NS:(c + 1) * NS, 0, :])
        m["sc"] = f(state_conv[0, c * NS:(c + 1) * NS].reshape(2 * NS, D))
        in_maps.append(m)
    res = run_bass_kernel_spmd(nc, in_maps, core_ids=list(range(N_CORES)))
    r = res.results
    y_prompt = np.stack([r[c]["yp"] for c in range(N_CORES)], axis=0).astype(np.float32)
    y_sample = np.concatenate([r[c]["ys"] for c in range(N_CORES)], axis=0).reshape(128, 1, D).astype(np.float32)
    sc_prompt = np.stack([r[c]["scp"] for c in range(N_CORES)], axis=0).reshape(1, N_CORES, 2, D).astype(np.float32)
    sc_sample = np.concatenate([r[c]["scs"].reshape(NS, 2, D) for c in range(N_CORES)], axis=0).reshape(1, 128, 2, D).astype(np.float32)
    sv_sample = np.concatenate([r[c]["svs"] for c in range(N_CORES)], axis=0).reshape(1, 128, 1, D).astype(np.float32)
    return (y_prompt, y_sample, sc_prompt, sc_sample, sv_sample)
```

```python
from contextlib import ExitStack

import numpy as np
import concourse.bass as bass
import concourse.mybir as mybir
from concourse.bass_utils import run_bass_kernel_spmd

F32 = mybir.dt.float32
F32R = mybir.dt.float32r
BF16 = mybir.dt.bfloat16
AF = mybir.ActivationFunctionType
ALU = mybir.AluOpType

EXACT = False
MMDT = F32 if EXACT else F32R
SAME_ENGINE_SYNC = True

N_CORES = 8
D = 1024
SEQ = 2048
NS = 16
NBLK = 4
NT = 512
EPS = 1e-5
RING = 4

ENGS = ("sync", "scalar", "vector", "gpsimd", "tensor")


class Buf:
    __slots__ = ("name", "writer", "readers")

    def __init__(self, name):
        self.name = name
        self.writer = None
        self.readers = []


class Op:
    __slots__ = ("eng", "fn", "deps", "dma_sem", "tok", "milestone", "idx")

    def __init__(self, eng, fn, dma_sem):
        self.eng = eng
        self.fn = fn
        self.deps = []
        self.dma_sem = dma_sem
        self.tok = None
        self.milestone = False


class Plan:
    def __init__(self):
        self.q = {e: [] for e in ENGS}
        self.dma_count = {}

    def add(self, eng, fn, reads=(), writes=(), dma_sem=None):
        op = Op(eng, fn, dma_sem)
        deps = []
        for b in reads:
            if b.writer is not None:
                deps.append(b.writer)
        for b in writes:
            if b.writer is not None:
                deps.append(b.writer)
            deps.extend(b.readers)
        for b in reads:
            b.readers.append(op)
        for b in writes:
            b.writer = op
            b.readers = []
        seen = set()
        last = {}
        for d in deps:
            if d is op or id(d) in seen:
                continue
            seen.add(id(d))
            if d.dma_sem is not None:
                op.deps.append(d)
                continue
            if d.eng == eng and (eng == "tensor" or not SAME_ENGINE_SYNC):
                continue
            if d.eng not in last or last[d.eng].idx < d.idx:
                last[d.eng] = d
        for d in last.values():
            op.deps.append(d)
            d.milestone = True
        if dma_sem is not None:
            key = id(dma_sem)
            self.dma_count[key] = self.dma_count.get(key, 0) + 16
            op.tok = (dma_sem, self.dma_count[key])
        op.idx = len(self.q[eng])
        self.q[eng].append(op)
        return op

    def unify(self, ops):
        v = max(o.tok[1] for o in ops)
        for o in ops:
            o.tok = (o.tok[0], v)

    def finalize(self, tick_sems):
        for e in ENGS:
            t = 0
            for op in self.q[e]:
                if op.dma_sem is None and op.milestone:
                    t += 1
                    op.tok = (tick_sems[e], t)

    def emit(self, e, eng):
        waited = {}
        for op in self.q[e]:
            need = {}
            for d in op.deps:
                s, v = d.tok
                if waited.get(id(s), 0) >= v:
                    continue
                if need.get(id(s), (None, 0))[1] < v:
                    need[id(s)] = (s, v)
            for s, v in need.values():
                eng.wait_ge(s, v)
                waited[id(s)] = v
            ins = op.fn(eng)
            if op.dma_sem is not None:
                ins.then_inc(op.dma_sem, 16)
            elif op.milestone:
                ins.then_inc(op.tok[0], 1)


class Rot:
    def __init__(self, items):
        self.items = items
        self.i = 0

    def next(self):
        it = self.items[self.i % len(self.items)]
        self.i += 1
        return it


def build_program():
    nc = bass.Bass("TRN2", target_bir_lowering=False)
    nc.dge_precook = False
    P = Plan()

    def din(name, shape, dt=F32):
        return nc.dram_tensor(name, list(shape), dt, kind="ExternalInput").ap()

    def dout(name, shape):
        return nc.dram_tensor(name, list(shape), F32, kind="ExternalOutput").ap()

    xp = din("xp", [SEQ, D])
    xs = din("xs", [NS, D])
    sc = din("sc", [NS * 2, D])
    gn_d = din("gn", [128, 8])
    cw_d = din("cw", [128, 8 * 3])
    wt_d = din("wt", [18, 128, 8 * 512], MMDT)
    wsT_d = din("wsT", [128, 8 * 128])
    bs_d = din("bs", [8 * 128])
    w00_d = din("w00", [8])
    b0_d = din("b0", [8])
    gv_d = din("gv", [D])
    gf_d = din("gf", [D])
    yp = dout("yp", [SEQ, D])
    ys = dout("ys", [NS, D])
    scp = dout("scp", [2, D])
    scs = dout("scs", [NS, 2 * D])
    svs = dout("svs", [NS, D])

    es = ExitStack()
    with es:
        def sb(name, shape, dt=F32):
            return es.enter_context(nc.sbuf_tensor(name, list(shape), dt))

        ring = [sb(f"ring{i}", [128, 8, 512], MMDT) for i in range(RING)]
        hT = sb("hT", [128, 8, NT + NS], MMDT)
        hTs = hT[:, :, NT:NT + NS]
        vn = sb("vn", [128, 4, D], MMDT)
        vs32 = sb("vs32", [NS, D], F32)
        mixT = sb("mixT", [128, 16, NT], MMDT)
        mixTs = sb("mixTs", [128, 16, NS], MMDT)
        tok = [sb(f"tok{i}", [128, D], F32) for i in range(5)]
        xin = [sb(f"xin{i}", [128, D], F32) for i in range(3)]
        junk = sb("junk", [128, D], BF16)
        fa_all = sb("fa_all", [128, 2, NT], F32)
        fa = [fa_all[:, i, :] for i in range(2)]
        fb = [sb(f"fb{i}", [128, NT + 2], F32) for i in range(2)]
        fc = [sb(f"fc{i}", [128, NT], F32) for i in range(2)]
        fd = [sb(f"fd{i}", [128, NT], F32) for i in range(2)]
        sa = [sb(f"sa{i}", [128, NS], F32) for i in range(2)]
        sbb = [sb(f"sbb{i}", [128, NS], F32) for i in range(2)]
        scc = [sb(f"scc{i}", [128, NS], F32) for i in range(2)]
        sd = [sb(f"sd{i}", [128, NS], F32) for i in range(2)]
        hbs = sb("hbs", [128, 8, NS], F32)
        stT = sb("stT", [128, 8, 2 * NS], F32)
        carry = sb("carry", [128, 8, 2], F32)
        gvbc = sb("gvbc", [128, D], F32)
        gfbc = sb("gfbc", [128, D], F32)
        WT = sb("WT", [128, 8, 128], MMDT)
        bbc = sb("bbc", [128, 8, 128], F32)
        onesr = sb("onesr", [1, 128], MMDT)
        ones32 = sb("ones32", [128, 128], F32)
        ident = sb("ident", [128, 128], F32)
        gn = sb("gnt", [128, 8], F32)
        cw = sb("cwt", [128, 8 * 3], F32)
        w00 = sb("w00t", [128, 8], F32)
        b0 = sb("b0t", [128, 8], F32)
        mhalf = sb("mhalf", [128, 1], F32)
        stat = sb("stat", [128, 18], F32)
        ps = es.enter_context(nc.psum_tensor("ps", [128, 8, 512], F32))

        sem = lambda n: es.enter_context(nc.semaphore(n))
        tick = {e: sem(f"tick_{e}") for e in ENGS if e != "sync"}
        s_ring = [sem(f"s_ring{i}") for i in range(RING)]
        s_const = sem("s_const")
        s_const2 = sem("s_const2")
        s_xin = [sem(f"s_xin{i}") for i in range(3)]
        s_tokld = [sem(f"s_tokld{i}") for i in range(5)]
        s_tokst = [sem(f"s_tokst{i}") for i in range(5)]
        s_out = sem("s_out")
        block = es.enter_context(nc.Block())

        B = Buf
        b_ring = [[B(f"ring{i}q{q}") for q in range(4)] for i in range(RING)]
        b_hT = [B(f"hT{t}") for t in range(4)]
        b_hTs = B("hTs")
        b_vn = [B(f"vn{t}") for t in range(4)]
        b_vs32 = B("vs32")
        b_mixT = [B(f"mixT{k}") for k in range(16)]
        b_mixTs = [B(f"mixTs{k}") for k in range(16)]
        b_tok = [B(f"tok{i}") for i in range(5)]
        b_xin = [B(f"xin{i}") for i in range(3)]
        b_junk = B("junk")
        b_fa = [B(f"fa{i}") for i in range(2)]
        b_fb = [B(f"fb{i}") for i in range(2)]
        b_fc = [B(f"fc{i}") for i in range(2)]
        b_fd = [B(f"fd{i}") for i in range(2)]
        b_sa = [B(f"sa{i}") for i in range(2)]
        b_sb = [B(f"sbb{i}") for i in range(2)]
        b_sc = [B(f"scc{i}") for i in range(2)]
        b_sd = [B(f"sd{i}") for i in range(2)]
        b_hbs = [B(f"hbs{j}") for j in range(8)]
        b_stT = B("stT")
        b_carry = [B(f"carry{j}") for j in range(8)]
        b_const = B("const")
        b_ps = [B(f"ps{i}") for i in range(8)]
        b_stat = [B(f"stat{i}") for i in range(8)]

        rF = Rot([0, 1, 2, 7])
        rT = Rot([4, 5, 6])
        BANK_S = 7
        BANK_M = 3
        r_fa, r_fb, r_fc, r_fd = Rot([0, 1]), Rot([0, 1]), Rot([0, 1]), Rot([0, 1])
        r_sa, r_sb, r_sc, r_sd = Rot([0, 1]), Rot([0, 1]), Rot([0, 1]), Rot([0, 1])
        r_xin = Rot([0, 1, 2])
        r_stat = Rot(list(range(8)))
        r_tokgv = Rot([0, 1])

        b_cd = {n: B("cd_" + n) for n in ("gn", "cw", "w00", "b0", "gv", "gf", "wsT", "bs")}
        c_ops = []

        c_ops2 = []

        def cdma(name, out, in_, late=False):
            (c_ops2 if late else c_ops).append(
                P.add("sync", lambda e, o=out, i_=in_: e.dma_start(out=o, in_=i_),
                      writes=[b_cd[name]], dma_sem=(s_const2 if late else s_const)))

        def const_dmas_early():
            cdma("gn", gn[:], gn_d)
            cdma("gv", gvbc[:], gv_d.partition_broadcast(128))

        def const_dmas_late():
            cdma("cw", cw[:], cw_d, late=True)
            cdma("w00", w00[:], w00_d.partition_broadcast(128), late=True)
            cdma("b0", b0[:], b0_d.partition_broadcast(128), late=True)
            cdma("gf", gfbc[:], gf_d.partition_broadcast(128), late=True)
            cdma("wsT", tok[4][:], wsT_d, late=True)
            cdma("bs", bbc[:].rearrange("p h t -> p (h t)"), bs_d.partition_broadcast(128), late=True)

        b_setup = B("setup")
        b_ident, b_WT = B("ident"), B("WT")

        def const_compute():
            P.add("gpsimd", lambda e: e.affine_select(out=WT[:].rearrange("p h t -> p (h t)"), in_=tok[4][:],
                                                      pattern=[[0, 8], [1, 128]], compare_op=ALU.is_ge,
                                                      fill=0.0, base=0, channel_multiplier=-1),
                  reads=[b_cd["wsT"]], writes=[b_WT, b_tok[4]])
            P.add("vector", lambda e: e.tensor_copy(out=onesr[:], in_=ones32[0:1, :]),
                  reads=[b_setup, b_ident, b_WT] + list(b_cd.values()), writes=[b_const])

        P.add("gpsimd", lambda e: e.memset(mhalf[:], -0.5), writes=[b_setup])
        P.add("gpsimd", lambda e: e.memset(ones32[:], 1.0), writes=[b_setup])
        P.add("gpsimd", lambda e: e.memset(carry[:], 0.0), writes=b_carry)
        P.add("scalar", lambda e: e.activation(out=stat[:, 16:17], in_=mhalf[:, 0:1], func=AF.Gelu),
              reads=[b_setup], writes=[B("warm")])
        P.add("gpsimd", lambda e: e.affine_select(out=ident[:], in_=ones32[:], pattern=[[1, 128]],
                                                  compare_op=ALU.is_equal, fill=0.0, base=0,
                                                  channel_multiplier=-1),
              reads=[b_setup], writes=[b_ident])

        wtiles = []
        for _b in range(NBLK):
            wtiles += [("v", 0), ("v", 1)]
            wtiles += [("B", j) for j in range(8)]
            wtiles += [("A", i) for i in range(4)]
            wtiles += [("o", 0, 0), ("o", 0, 1), ("o", 1, 0), ("o", 1, 1)]
        n_w = len(wtiles)
        w_issued = [0]

        def issue_weight(idx):
            slot = idx % RING
            rt = ring[slot]
            P.add("sync", lambda e, rt=rt, idx=idx: e.dma_start(out=rt[:].rearrange("p k c -> p (k c)"),
                                                               in_=wt_d[idx % 18]),
                  writes=b_ring[slot], dma_sem=s_ring[slot])

        def prefetch(upto):
            while w_issued[0] < min(upto, n_w):
                issue_weight(w_issued[0])
                w_issued[0] += 1

        w_cursor = [0]

        def next_weight(expect):
            idx = w_cursor[0]
            assert wtiles[idx][0] == expect, (wtiles[idx], expect)
            assert idx < w_issued[0]
            w_cursor[0] += 1
            return idx % RING

        def release_weights():
            prefetch(w_cursor[0] + RING)

        class Group:
            pass

        def make_prompt_group(b):
            g = Group()
            g.kind = "P"
            g.n = NT
            g.tiles = [(t * 128, 128) for t in range(4)]
            g.hT = hT
            g.b_hT = b_hT
            g.mixT = mixT
            g.b_mixT = b_mixT
            g.x_rows = lambda t: xp[b * NT + t * 128:b * NT + (t + 1) * 128, :]
            g.y_rows = lambda t: yp[b * NT + t * 128:b * NT + (t + 1) * 128, :]
            g.tokidx = [0, 1, 2, 3]
            g.blk = b
            return g

        def make_sample_group():
            g = Group()
            g.kind = "S"
            g.n = NS
            g.tiles = [(0, NS)]
            g.hT = hTs
            g.b_hT = [b_hTs]
            g.mixT = mixTs
            g.b_mixT = b_mixTs
            g.x_rows = lambda t: xs[:, :]
            g.y_rows = lambda t: ys[:, :]
            g.tokidx = [4]
            g.blk = 0
            return g

        def rms_inv(src_ap, nr, b_src, ncols):
            si = r_stat.next()
            ssq = stat[:nr, 2 * si:2 * si + 1]
            rinv = stat[:nr, 2 * si + 1:2 * si + 2]
            bst = b_stat[si]
            P.add("scalar", lambda e: e.activation(out=junk[:nr, 0:ncols], in_=src_ap, func=AF.Square,
                                                   accum_out=ssq),
                  reads=(b_src if isinstance(b_src, list) else [b_src]), writes=[b_junk, bst])
            P.add("gpsimd", lambda e: e.tensor_scalar(out=ssq, in0=ssq, scalar1=1.0 / ncols, scalar2=EPS,
                                                      op0=ALU.mult, op1=ALU.add),
                  reads=[bst], writes=[bst])
            P.add("gpsimd", lambda e: e.tensor_tensor(out=rinv, in0=ssq, in1=mhalf[:nr, :], op=ALU.pow),
                  reads=[bst, b_setup], writes=[bst])
            return rinv, bst

        class XSrc:
            def __init__(self, t, b, sm):
                self.t, self.s = t, sm
                self.b = list(b) if isinstance(b, (list, tuple)) else [b]

        xsrc_xin = [XSrc(xin[i], b_xin[i], s_xin[i]) for i in range(3)]
        xsrc_tok = [XSrc(tok[i], b_tok[i], s_tokld[i]) for i in range(5)]
        s_vnld = sem("s_vnld")
        xsrc_vn3 = XSrc(fa_all[:].rearrange("p a n -> p (a n)"), [b_fa[0], b_fa[1]], s_vnld)

        def xprep_load(g, t, src=None, queue="sync"):
            xs_ = xsrc_xin[r_xin.next()] if src is None else src
            P.add(queue, lambda e: e.dma_start(out=xs_.t[:g.tiles[t][1], :], in_=g.x_rows(t)),
                  writes=xs_.b, dma_sem=xs_.s)
            return xs_

        pre_state = {}

        def xprep_pre_a(g, t, xi):
            c0, nr = g.tiles[t]
            xa = xi.t
            pre_state[(id(g), t)] = rms_inv(xa[:nr, :], nr, xi.b, D)

        def xprep_pre_b(g, t, xi):
            c0, nr = g.tiles[t]
            xa = xi.t
            rinv, bst = pre_state.pop((id(g), t))
            P.add("scalar", lambda e: e.activation(out=xa[:nr, :], in_=xa[:nr, :], func=AF.Copy, scale=rinv),
                  reads=[bst] + xi.b, writes=xi.b)

        def xprep_pre(g, t, xi):
            xprep_pre_a(g, t, xi)
            xprep_pre_b(g, t, xi)

        def xprep_post(g, t, xi):
            c0, nr = g.tiles[t]
            xa = xi.t
            for hf in range(2):
                bank = rF.next()
                for q in range(4):
                    k = hf * 4 + q
                    P.add("tensor", lambda e, k=k, q=q, bank=bank: e.transpose(
                        out=ps[:, bank, q * 128:q * 128 + nr], in_=xa[:nr, k * 128:(k + 1) * 128],
                        identity=ident[:nr, :nr]),
                        reads=xi.b + [b_ident], writes=[b_ps[bank]])
                P.add("vector", lambda e, hf=hf, bank=bank: e.tensor_tensor(
                    out=g.hT[:, hf * 4:hf * 4 + 4, c0:c0 + nr],
                    in0=ps[:, bank, :].rearrange("p (q t) -> p q t", q=4)[:, :, 0:nr],
                    in1=gn[:, hf * 4:hf * 4 + 4].unsqueeze(2).to_broadcast([128, 4, nr]), op=ALU.mult),
                    reads=[b_ps[bank], b_cd["gn"]], writes=[g.b_hT[t]])

        def xprep_compute(g, t, xi):
            xprep_pre(g, t, xi)
            xprep_post(g, t, xi)

        def load_state_dma():
            P.add("sync", lambda e: e.dma_start(out=tok[2][:2 * NS, :], in_=sc),
                  writes=[b_tok[2]], dma_sem=s_tokld[2])
            sc3 = sc.rearrange("(s r) d -> s r d", r=2)
            P.add("sync", lambda e: e.dma_start(out=scs[:, 0:D], in_=sc3[:, 1, :]), dma_sem=s_out)

        def load_state():
            for hf in range(2):
                bank = rF.next()
                for q in range(4):
                    k = hf * 4 + q
                    P.add("tensor", lambda e, k=k, q=q, bank=bank: e.transpose(
                        out=ps[:, bank, q * 128:q * 128 + 2 * NS], in_=tok[2][:2 * NS, k * 128:(k + 1) * 128],
                        identity=ident[:2 * NS, :2 * NS]),
                        reads=[b_tok[2], b_ident], writes=[b_ps[bank]])
                for q in range(4):
                    k = hf * 4 + q
                    P.add("vector", lambda e, k=k, q=q, bank=bank: e.tensor_copy(
                        out=stT[:, k, :], in_=ps[:, bank, q * 128:q * 128 + 2 * NS]),
                        reads=[b_ps[bank]], writes=[b_stT])

        def v_tile(g, t, slots):
            c0, nr = g.tiles[t]
            gi = r_tokgv.next()
            gvt = tok[gi]
            for half in range(2):
                bank = rT.next()
                rt = ring[slots[half]]
                for k in range(8):
                    P.add("tensor", lambda e, k=k, bank=bank, rt=rt: e.matmul(
                        ps[:nr, bank, :], lhsT=g.hT[:, k, c0:c0 + nr], rhs=rt[:, k, :],
                        start=(k == 0), stop=(k == 7)),
                        reads=[g.b_hT[t]] + b_ring[slots[half]], writes=[b_ps[bank]])
                P.add("scalar", lambda e, half=half, bank=bank: e.activation(
                    out=gvt[:nr, half * 512:(half + 1) * 512], in_=ps[:nr, bank, :], func=AF.Gelu),
                    reads=[b_ps[bank]], writes=[b_tok[gi]])
            rinv, bst = rms_inv(gvt[:nr, :], nr, b_tok[gi], D)
            if g.kind == "P":
                P.add("vector", lambda e: e.scalar_tensor_tensor(
                    out=vn[:nr, t, :], in0=gvt[:nr, :], scalar=rinv, in1=gvbc[:nr, :],
                    op0=ALU.mult, op1=ALU.mult),
                    reads=[b_tok[gi], bst, b_cd["gv"]], writes=[b_vn[t]])
            else:
                P.add("vector", lambda e: e.scalar_tensor_tensor(
                    out=vs32[:nr, :], in0=gvt[:nr, :], scalar=rinv, in1=gvbc[:nr, :],
                    op0=ALU.mult, op1=ALU.mult),
                    reads=[b_tok[gi], bst, b_cd["gv"]], writes=[b_vs32])
                P.add("sync", lambda e: e.dma_start(out=svs, in_=vs32[:, :]),
                      reads=[b_vs32], dma_sem=s_out)

        def v_phase(groups, slots):
            for g in groups:
                for t in range(len(g.tiles)):
                    v_tile(g, t, slots)

        def feat_mm(g, slot, q, col0):
            bank = rF.next()
            rt = ring[slot]
            for k in range(8):
                P.add("tensor", lambda e, k=k: e.matmul(
                    ps[:, bank, 0:g.n], lhsT=rt[:, k, col0:col0 + 128], rhs=g.hT[:, k, 0:g.n],
                    start=(k == 0), stop=(k == 7)),
                    reads=g.b_hT + [b_ring[slot][q]], writes=[b_ps[bank]])
            return bank

        with_sample = [False]
        samp_piece = {}

        rF2 = Rot([0, 1, 2, 7, 4, 5])

        def feat_mm2(slot, q, col0):
            bx, by = rF2.next(), rF2.next()
            rt = ring[slot]
            for k in range(8):
                P.add("tensor", lambda e, k=k: e.matmul(
                    ps[:, bx, 0:256], lhsT=rt[:, k, col0:col0 + 128], rhs=hT[:, k, 0:256],
                    start=(k == 0), stop=(k == 7)),
                    reads=[b_hT[0], b_hT[1], b_ring[slot][q]], writes=[b_ps[bx]])
                P.add("tensor", lambda e, k=k: e.matmul(
                    ps[:, by, 0:256 + NS], lhsT=rt[:, k, col0:col0 + 128], rhs=hT[:, k, 256:NT + NS],
                    start=(k == 0), stop=(k == 7)),
                    reads=[b_hT[2], b_hT[3], b_hTs, b_ring[slot][q]], writes=[b_ps[by]])
            return bx, by

        def seg_src(g, slot, q, col0):
            if g.kind == "P":
                if with_sample[0]:
                    bx, by = feat_mm2(slot, q, col0)
                    samp_piece[q] = (ps[:, by, 256:256 + NS], b_ps[by])
                    return [(ps[:, bx, 0:256], b_ps[bx], 0, 256), (ps[:, by, 0:256], b_ps[by], 256, 512)]
                bank = feat_mm(g, slot, q, col0)
                return [(ps[:, bank, 0:g.n], b_ps[bank], 0, g.n)]
            sap, sbuf = samp_piece.pop(q)
            return [(sap, sbuf, 0, NS)]

        def interleave(gens):
            gens = list(gens)
            while gens:
                for gen in list(gens):
                    try:
                        next(gen)
                    except StopIteration:
                        gens.remove(gen)

        def b_side(groups, slot, j):
            interleave([b_side_g(g, slot, j) for g in groups])

        def b_side_g(g, slot, j):
            n = g.n
            isP = g.kind == "P"
            ia = (r_fa if isP else r_sa).next()
            ib = (r_fb if isP else r_sb).next()
            ic = (r_fc if isP else r_sc).next()
            idd = (r_fd if isP else r_sd).next()
            ta, b_ta = (fa[ia], b_fa[ia]) if isP else (sa[ia], b_sa[ia])
            tc, b_tc = (fc[ic], b_fc[ic]) if isP else (scc[ic], b_sc[ic])
            td, b_td = (fd[idd], b_fd[idd]) if isP else (sd[idd], b_sd[idd])
            w0, w1, w2 = (cw[:, 3 * j + i:3 * j + i + 1] for i in range(3))
            for sap, sbuf, c0, c1 in seg_src(g, slot, 0, 0):
                P.add("scalar", lambda e, sap=sap, c0=c0, c1=c1: e.activation(out=ta[:, c0:c1], in_=sap, func=AF.Copy),
                      reads=[sbuf], writes=[b_ta])
            yield
            pieces = seg_src(g, slot, 2, 256)
            if isP:
                tb, b_tb = fb[ib], b_fb[ib]
                P.add("vector", lambda e: e.tensor_copy(out=tb[:, 0:2], in_=carry[:, j, :]),
                      reads=[b_carry[j]], writes=[b_tb])
                for sap, sbuf, c0, c1 in pieces:
                    P.add("vector", lambda e, sap=sap, c0=c0, c1=c1: e.tensor_tensor(
                        out=tb[:, 2 + c0:2 + c1], in0=sap, in1=ta[:, c0:c1], op=ALU.mult),
                        reads=[sbuf, b_ta], writes=[b_tb])
                h0, h1, h2 = tb[:, 0:n], tb[:, 1:n + 1], tb[:, 2:n + 2]
                rd = [b_tb]
            else:
                for sap, sbuf, c0, c1 in pieces:
                    P.add("vector", lambda e, sap=sap: e.tensor_tensor(out=hbs[:, j, :], in0=sap,
                                                                     in1=ta[:, 0:n], op=ALU.mult),
                          reads=[sbuf, b_ta], writes=[b_hbs[j]])
                st3 = stT[:, j, :].rearrange("p (s r) -> p s r", r=2)
                h0, h1, h2 = st3[:, :, 0], st3[:, :, 1], hbs[:, j, :]
                rd = [b_hbs[j], b_stT]
            yield
            if isP:
                P.add("vector", lambda e: e.tensor_copy(out=carry[:, j, :], in_=tb[:, n:n + 2]),
                      reads=[b_tb], writes=[b_carry[j]])
            P.add("vector", lambda e: e.tensor_scalar(out=tc[:, 0:n], in0=h0, scalar1=w0, scalar2=None,
                                                      op0=ALU.mult),
                  reads=rd + [b_const], writes=[b_tc])
            P.add("vector", lambda e: e.scalar_tensor_tensor(out=tc[:, 0:n], in0=h1, scalar=w1,
                                                             in1=tc[:, 0:n], op0=ALU.mult, op1=ALU.add),
                  reads=rd + [b_const, b_tc], writes=[b_tc])
            P.add("vector", lambda e: e.scalar_tensor_tensor(out=tc[:, 0:n], in0=h2, scalar=w2,
                                                             in1=tc[:, 0:n], op0=ALU.mult, op1=ALU.add),
                  reads=rd + [b_const, b_tc], writes=[b_tc])
            for sap, sbuf, c0, c1 in seg_src(g, slot, 3, 384):
                P.add("scalar", lambda e, sap=sap, c0=c0, c1=c1: e.activation(out=td[:, c0:c1], in_=sap, func=AF.Tanh,
                                                                              scale=0.5),
                      reads=[sbuf], writes=[b_td])
                P.add("vector", lambda e, sap=sap, c0=c0, c1=c1: e.scalar_tensor_tensor(
                    out=td[:, c0:c1], in0=td[:, c0:c1], scalar=1.0, in1=sap, op0=ALU.add, op1=ALU.mult),
                    reads=[sbuf, b_td], writes=[b_td])
            yield
            for sap, sbuf, c0, c1 in seg_src(g, slot, 1, 128):
                P.add("vector", lambda e, sap=sap, c0=c0, c1=c1: e.tensor_tensor(
                    out=tc[:, c0:c1], in0=sap, in1=tc[:, c0:c1], op=ALU.mult),
                    reads=[sbuf, b_tc], writes=[b_tc])
            P.add("vector", lambda e: e.scalar_tensor_tensor(out=g.mixT[:, 8 + j, 0:n], in0=tc[:, 0:n],
                                                             scalar=0.5, in1=td[:, 0:n], op0=ALU.mult,
                                                             op1=ALU.mult),
                  reads=[b_tc, b_td], writes=[g.b_mixT[8 + j]])

        def a_side(groups, slot, h):
            interleave([a_side_g(g, slot, h) for g in groups])

        def a_side_g(g, slot, h):
            hl = h % 2
            n = g.n
            isP = g.kind == "P"
            ia = (r_fa if isP else r_sa).next()
            idd = (r_fd if isP else r_sd).next()
            ta, b_ta = (fa[ia], b_fa[ia]) if isP else (sa[ia], b_sa[ia])
            td, b_td = (fd[idd], b_fd[idd]) if isP else (sd[idd], b_sd[idd])
            for sap, sbuf, c0, c1 in seg_src(g, slot, hl, hl * 128):
                P.add("scalar", lambda e, sap=sap, c0=c0, c1=c1: e.activation(out=ta[:, c0:c1], in_=sap, func=AF.Gelu),
                      reads=[sbuf], writes=[b_ta])
            yield
            for sap, sbuf, c0, c1 in seg_src(g, slot, 2 + hl, 256 + hl * 128):
                P.add("scalar", lambda e, sap=sap, c0=c0, c1=c1: e.activation(out=td[:, c0:c1], in_=sap, func=AF.Tanh,
                                                                              scale=0.5),
                      reads=[sbuf], writes=[b_td])
                P.add("vector", lambda e, sap=sap, c0=c0, c1=c1: e.scalar_tensor_tensor(
                    out=td[:, c0:c1], in0=td[:, c0:c1], scalar=1.0, in1=sap, op0=ALU.add, op1=ALU.mult),
                    reads=[sbuf, b_td], writes=[b_td])
            yield
            if isP:
                for c in range(4):
                    P.add("tensor", lambda e, c=c: e.matmul(
                        ps[:, BANK_M, c * 128:(c + 1) * 128], lhsT=vn[:, c, h * 128:(h + 1) * 128],
                        rhs=WT[:, h, :], start=True, stop=True),
                        reads=[b_vn[c], b_const], writes=[b_ps[BANK_M]])
                ic = r_fc.next()
                tc, b_tc = fc[ic], b_fc[ic]
                P.add("vector", lambda e: e.tensor_tensor(
                    out=tc[:, 0:n].rearrange("p (c t) -> p c t", c=4),
                    in0=ps[:, BANK_M, 0:n].rearrange("p (c t) -> p c t", c=4),
                    in1=bbc[:, h, :].unsqueeze(1).to_broadcast([128, 4, 128]), op=ALU.add),
                    reads=[b_ps[BANK_M], b_const], writes=[b_tc])
                P.add("vector", lambda e: e.tensor_tensor(out=ta[:, 0:n], in0=tc[:, 0:n],
                                                          in1=ta[:, 0:n], op=ALU.mult),
                      reads=[b_tc, b_ta], writes=[b_ta])
            else:
                cbank = 6
                cmap = ps[:, cbank, 0:NS]
                P.add("tensor", lambda e: e.transpose(
                    out=cmap, in_=vs32[:n, h * 128:(h + 1) * 128], identity=ident[:n, :n]),
                    reads=[b_vs32, b_const], writes=[b_ps[cbank]])
                ic = r_sc.next()
                tc, b_tc = scc[ic], b_sc[ic]
                P.add("vector", lambda e: e.tensor_scalar(out=tc[:, 0:n], in0=cmap,
                                                          scalar1=w00[:, h:h + 1], scalar2=b0[:, h:h + 1],
                                                          op0=ALU.mult, op1=ALU.add),
                      reads=[b_ps[cbank], b_const], writes=[b_tc])
                P.add("vector", lambda e: e.tensor_tensor(out=ta[:, 0:n], in0=tc[:, 0:n], in1=ta[:, 0:n],
                                                          op=ALU.mult),
                      reads=[b_tc, b_ta], writes=[b_ta])
            P.add("vector", lambda e: e.scalar_tensor_tensor(out=g.mixT[:, h, 0:n], in0=ta[:, 0:n], scalar=0.5,
                                                             in1=td[:, 0:n], op0=ALU.mult, op1=ALU.mult),
                  reads=[b_ta, b_td], writes=[g.b_mixT[h]])

        def load_resid(groups):
            for g in groups:
                for t, (c0, nr) in enumerate(g.tiles):
                    ti = g.tokidx[t]
                    P.add("sync", lambda e, t=t, ti=ti, nr=nr, g=g: e.dma_start(out=tok[ti][:nr, :], in_=g.x_rows(t)),
                          writes=[b_tok[ti]], dma_sem=s_tokld[ti])

        def out_tile(g, t, half, slots):
            c0, nr = g.tiles[t]
            ti = g.tokidx[t]
            bank = rT.next()
            korder = list(range(8, 16)) + list(range(8))
            for i_, k in enumerate(korder):
                rt = ring[slots[k // 8]]
                P.add("tensor", lambda e, k=k, rt=rt, i_=i_: e.matmul(
                    ps[:nr, bank, :], lhsT=g.mixT[:, k, c0:c0 + nr], rhs=rt[:, k % 8, :],
                    start=(i_ == 0), stop=(i_ == 15)),
                    reads=[g.b_mixT[k]] + b_ring[slots[k // 8]], writes=[b_ps[bank]])
            yt = tok[ti]
            P.add("vector", lambda e: e.tensor_tensor(
                out=yt[:nr, half * 512:(half + 1) * 512], in0=ps[:nr, bank, :],
                in1=yt[:nr, half * 512:(half + 1) * 512], op=ALU.add),
                reads=[b_ps[bank], b_tok[ti]], writes=[b_tok[ti]])
            if half == 1:
                rinv, bst = rms_inv(yt[:nr, :], nr, b_tok[ti], D)

                def finish():
                    P.add("vector", lambda e: e.scalar_tensor_tensor(
                        out=yt[:nr, :], in0=yt[:nr, :], scalar=rinv, in1=gfbc[:nr, :],
                        op0=ALU.mult, op1=ALU.mult),
                        reads=[b_tok[ti], bst, b_const], writes=[b_tok[ti]])
                    P.add("sync", lambda e: e.dma_start(out=g.y_rows(t), in_=yt[:nr, :]),
                          reads=[b_tok[ti]], dma_sem=s_tokst[ti])
                return finish
            return None

        def out_proj_half(groups, half, slots, between=None):
            pending = None
            for g in groups:
                for t in range(len(g.tiles)):
                    fin = out_tile(g, t, half, slots)
                    if pending is not None:
                        pending()
                    pending = fin
                    if between is not None:
                        between(g, t)
            if pending is not None:
                pending()

        def emit_state_outputs_sample():
            banks = [rT.next(), rT.next()]
            for k in range(8):
                bank = banks[k // 4]
                q = k % 4
                P.add("tensor", lambda e, k=k, q=q, bank=bank: e.transpose(
                    out=ps[:NS, bank, q * 128:(q + 1) * 128], in_=hbs[:, k, :], identity=ident[:, :]),
                    reads=[b_hbs[k], b_const], writes=[b_ps[bank]])
            for hf in range(2):
                P.add("scalar", lambda e, hf=hf: e.activation(out=tok[4][:NS, hf * 512:(hf + 1) * 512],
                                                              in_=ps[:NS, banks[hf], :], func=AF.Copy),
                      reads=[b_ps[banks[hf]]], writes=[b_tok[4]])
            P.add("sync", lambda e: e.dma_start(out=scs[:, D:2 * D], in_=tok[4][:NS, :]),
                  reads=[b_tok[4]], dma_sem=s_tokst[4])

        def emit_state_outputs_prompt():
            banks = [rT.next(), rT.next()]
            for k in range(8):
                bank = banks[k // 4]
                q = k % 4
                P.add("tensor", lambda e, k=k, q=q, bank=bank: e.transpose(
                    out=ps[:2, bank, q * 128:(q + 1) * 128], in_=carry[:, k, :], identity=ident[:, :]),
                    reads=[b_carry[k], b_const], writes=[b_ps[bank]])
            for hf in range(2):
                P.add("scalar", lambda e, hf=hf: e.activation(out=tok[4][:2, hf * 512:(hf + 1) * 512],
                                                              in_=ps[:2, banks[hf], :], func=AF.Copy),
                      reads=[b_ps[banks[hf]]], writes=[b_tok[4]])
            P.add("sync", lambda e: e.dma_start(out=scp, in_=tok[4][:2, :]),
                  reads=[b_tok[4]], dma_sem=s_tokst[4])

        gS = make_sample_group()
        g0 = make_prompt_group(0)
        const_dmas_early()
        xi_s = xprep_load(gS, 0, xsrc_tok[3])
        pend = [xprep_load(g0, 0), xprep_load(g0, 1), xprep_load(g0, 2, xsrc_tok[0]), xprep_load(g0, 3, xsrc_tok[1])]
        load_state_dma()
        prefetch(2)
        const_dmas_late()
        P.unify(c_ops)
        P.unify(c_ops2)
        prefetch(RING)
        xprep_pre_a(gS, 0, xi_s)
        xprep_pre_a(g0, 0, pend[0])
        xprep_pre_b(gS, 0, xi_s)
        xprep_pre_a(g0, 1, pend[1])
        xprep_pre_b(g0, 0, pend[0])
        xprep_post(gS, 0, xi_s)
        for t in range(4):
            if t + 2 < 4:
                xprep_pre_a(g0, t + 2, pend[t + 2])
            if t + 1 < 4:
                xprep_pre_b(g0, t + 1, pend[t + 1])
            xprep_post(g0, t, pend[t])
        load_state()
        const_compute()

        for b in range(NBLK):
            gP = make_prompt_group(b)
            groups = [gP, gS] if b == 0 else [gP]
            with_sample[0] = (b == 0)
            s0 = next_weight("v")
            s1 = next_weight("v")
            v_phase(groups, [s0, s1])
            release_weights()
            for j in range(8):
                s = next_weight("B")
                b_side(groups, s, j)
                release_weights()
                if j == 3 and b + 1 < NBLK:
                    gN = make_prompt_group(b + 1)
                    pend = [xprep_load(gN, 0), xprep_load(gN, 1), xprep_load(gN, 2)]
                if j == 5:
                    load_resid([gP])
            if b == 0:
                emit_state_outputs_sample()
                load_resid([gS])
            if b == NBLK - 1:
                emit_state_outputs_prompt()
            for i in range(4):
                s = next_weight("A")
                a_side(groups, s, 2 * i)
                a_side(groups, s, 2 * i + 1)
                release_weights()
                if b + 1 < NBLK:
                    if i == 0:
                        for t_ in range(len(pend)):
                            xprep_pre_a(gN, t_, pend[t_])
                    else:
                        xprep_pre_b(gN, i - 1, pend[i - 1])
            if b + 1 < NBLK:
                pend.append(xprep_load(gN, 3, xsrc_vn3))
                xprep_pre_a(gN, 3, pend[3])
            so = [next_weight("o"), next_weight("o")]
            out_proj_half(groups, 0, so)
            release_weights()
            if b + 1 < NBLK:
                xprep_pre_b(gN, 3, pend[3])
            so = [next_weight("o"), next_weight("o")]
            if b + 1 < NBLK:
                def between(g, t, gN=gN, pend=pend):
                    if g.kind != "P":
                        return
                    xprep_post(gN, t, pend[t])
                out_proj_half(groups, 1, so, between)
            else:
                out_proj_half(groups, 1, so)
            release_weights()

        P.finalize(tick)
        final_waits = [(s_out, P.dma_count.get(id(s_out), 0))]
        for s in s_tokst + s_xin:
            final_waits.append((s, P.dma_count.get(id(s), 0)))

        @block.sync
        def _(eng):
            P.emit("sync", eng)
            for s, v in final_waits:
                if v > 0:
                    eng.wait_ge(s, v)

        @block.scalar
        def _(eng):
            P.emit("scalar", eng)

        @block.vector
        def _(eng):
            P.emit("vector", eng)

        @block.gpsimd
        def _(eng):
            P.emit("gpsimd", eng)

        @block.tensor
        def _(eng):
            P.emit("tensor", eng)

    return nc


_NC_CACHE = {}


def kernel(x_prompt, x_sample, state_conv, g_norm, w_in, w_s, b_s, g_v, conv_w, w_out, g_final):
    f = lambda a: np.ascontiguousarray(np.asarray(a, dtype=np.float32))
    x_prompt, x_sample, state_conv = f(x_prompt), f(x_sample), f(state_conv)
    if "nc" not in _NC_CACHE:
        _NC_CACHE["nc"] = build_program()
    nc = _NC_CACHE["nc"]
    gn = f(np.asarray(g_norm)[0].reshape(8, 128).T)
    cw = f(np.asarray(conv_w)[0].reshape(3, 8, 128).transpose(2, 1, 0).reshape(128, 24))
    wsT = f(np.asarray(w_s)[0].transpose(2, 0, 1).reshape(128, 8 * 128))
    bs = f(np.asarray(b_s)[0].reshape(8 * 128))
    w00 = f(np.asarray(w_s)[0, :, 0, 0])
    b0 = f(np.asarray(b_s)[0, :, 0])
    wi = np.asarray(w_in, dtype=np.float32)[0]
    wo = np.asarray(w_out, dtype=np.float32)[0]
    tiles = []
    for half in range(2):
        tiles.append(wi[:, D + half * 512:D + (half + 1) * 512])
    for j in range(8):
        tiles.append(np.concatenate([wi[:, 3 * D + sg * D + j * 128:3 * D + sg * D + (j + 1) * 128] for sg in range(4)], axis=1))
    for i in range(4):
        tiles.append(np.concatenate([wi[:, i * 256:(i + 1) * 256], wi[:, 2 * D + i * 256:2 * D + (i + 1) * 256]], axis=1))
    for half in range(2):
        for kh in range(2):
            tiles.append(wo[kh * D:(kh + 1) * D, half * 512:(half + 1) * 512])
    wt = np.ascontiguousarray(np.stack([t_.reshape(8, 128, 512).transpose(1, 0, 2).reshape(128, 8 * 512) for t_ in tiles], axis=0))
    shared = {
        "gn": gn, "cw": cw, "wt": wt,
        "wsT": wsT, "bs": bs, "w00": w00, "b0": b0, "gv": f(np.asarray(g_v)[0]), "gf": f(g_final),
    }
    in_maps = []
    for c in range(N_CORES):
        m = dict(shared)
        m["xp"] = x_prompt[c]
        m["xs"] = f(x_sample[c * NS:(c + 1) * NS, 0, :])
        m["sc"] = f(state_conv[0, c * NS:(c + 1) * NS].reshape(2 * NS, D))
        in_maps.append(m)
    res = run_bass_kernel_spmd(nc, in_maps, core_ids=list(range(N_CORES)))
    r = res.results
    y_prompt = np.stack([r[c]["yp"] for c in range(N_CORES)], axis=0).astype(np.float32)
    y_sample = np.concatenate([r[c]["ys"] for c in range(N_CORES)], axis=0).reshape(128, 1, D).astype(np.float32)
    sc_prompt = np.stack([r[c]["scp"] for c in range(N_CORES)], axis=0).reshape(1, N_CORES, 2, D).astype(np.float32)
    sc_sample = np.concatenate([r[c]["scs"].reshape(NS, 2, D) for c in range(N_CORES)], axis=0).reshape(1, 128, 2, D).astype(np.float32)
    sv_sample = np.concatenate([r[c]["svs"] for c in range(N_CORES)], axis=0).reshape(1, 128, 1, D).astype(np.float32)
    return (y_prompt, y_sample, sc_prompt, sc_sample, sv_sample)
```
